# Optimizing a Trainium2 kernel written in Bass

```python
import jax, jax.numpy as jnp
from jax import lax
import numpy as np

D_MODEL = 2048
BATCH = 4
SEQ = 8192
DEPTH = 2

GRID_W = 64
CTX_LEN = 256
MIX_WIDTH = D_MODEL
FOURIER_WIDTH = MIX_WIDTH // 2
LRU_WIDTH = MIX_WIDTH - FOURIER_WIDTH
FOURIER_GROUPS = 4
FOURIER_GROUP_DIM = FOURIER_WIDTH // FOURIER_GROUPS
LRU_HEADS = 4
LRU_HEAD_DIM = LRU_WIDTH // LRU_HEADS
CONV_WIDTH = 4
CONV_LEFT = 2
LRU_C = 8.0
FFN_MULT = 256
D_FF = ((8 * D_MODEL + 3 * FFN_MULT - 1) // (3 * FFN_MULT)) * FFN_MULT
IN_COLS = FOURIER_WIDTH + 2 * LRU_WIDTH
N_MOD = 6
N_NORMS = 4
EPS = 1e-6

kernel_name = "hybrid_fourier_rglru_dit_block"


def rms_norm(x, g):
    xf = x.astype(jnp.float32)
    y = xf * lax.rsqrt(jnp.mean(xf * xf, axis=-1, keepdims=True) + EPS)
    return (y * g.astype(jnp.float32)).astype(x.dtype)


def modulate(h, shift, scale):
    return h * (1 + scale) + shift


def fourier_grid(u, w_four):
    bsz, length, _ = u.shape
    rows = length // GRID_W
    uf = u.astype(jnp.float32).reshape(bsz, rows, GRID_W, FOURIER_GROUPS, FOURIER_GROUP_DIM)
    z = jnp.fft.fftn(uf, axes=(1, 2, 4), norm="ortho").real
    z = z.reshape(bsz, length, FOURIER_GROUPS, FOURIER_GROUP_DIM).astype(u.dtype)
    return jnp.einsum("blgi,gij->blgj", z, w_four).reshape(bsz, length, FOURIER_WIDTH)


def fourier_seq(u, w_four):
    bsz, length, _ = u.shape
    uf = u.astype(jnp.float32).reshape(bsz, length, FOURIER_GROUPS, FOURIER_GROUP_DIM)
    z = jnp.fft.fftn(uf, axes=(1, 3), norm="ortho").real.astype(u.dtype)
    return jnp.einsum("blgi,gij->blgj", z, w_four).reshape(bsz, length, FOURIER_WIDTH)


def centred_depthwise_conv(u, w_conv, b_conv):
    length = u.shape[1]
    up = jnp.pad(u, ((0, 0), (CONV_LEFT, CONV_WIDTH - 1 - CONV_LEFT), (0, 0)))
    out = b_conv
    for k in range(CONV_WIDTH):
        out = out + w_conv[k] * up[:, k:k + length]
    return out


def rglru_coeffs(u, w_gates, b_gates, lam):
    bsz, length, _ = u.shape
    uh = u.reshape(bsz, length, LRU_HEADS, LRU_HEAD_DIM)
    gates = jnp.einsum("blhi,ghij->gblhj", uh, w_gates).reshape(2, bsz, length, LRU_WIDTH)
    gates = gates.astype(jnp.float32) + b_gates[:, None, None, :].astype(jnp.float32)
    r = jax.nn.sigmoid(gates[0])
    i = jax.nn.sigmoid(gates[1])
    log_a = -LRU_C * r * jax.nn.softplus(-lam.astype(jnp.float32))
    a = jnp.exp(log_a)
    x_in = jnp.sqrt(-jnp.expm1(2.0 * log_a)) * (i * u.astype(jnp.float32))
    return a, x_in


def _affine_combine(e1, e2):
    a1, b1 = e1
    a2, b2 = e2
    return a1 * a2, a2 * b1 + b2


def linear_scan(a, bx, h0):
    bx = bx.at[:, 0].add(a[:, 0] * h0)
    _, h = lax.associative_scan(_affine_combine, (a, bx), axis=1)
    return h


def mixing_sublayer(h_lat, h_ctx, w_in, w_four, conv_w, conv_b, lru_w, lru_b, lru_lam, w_out, with_ctx_out):
    split_at = (FOURIER_WIDTH, FOURIER_WIDTH + LRU_WIDTH)
    f_lat, r_lat, g_lat = jnp.split(h_lat @ w_in, split_at, axis=-1)
    f_ctx, r_ctx, g_ctx = jnp.split(h_ctx @ w_in, split_at, axis=-1)
    r_lat = centred_depthwise_conv(r_lat, conv_w, conv_b)
    r_ctx = centred_depthwise_conv(r_ctx, conv_w, conv_b)
    rec_lat, rec_ctx = [], []
    for d in range(2):
        a_c, b_c = rglru_coeffs(r_ctx, lru_w[d], lru_b[d], lru_lam[d])
        a_l, b_l = rglru_coeffs(r_lat, lru_w[d], lru_b[d], lru_lam[d])
        if d == 1:
            a_c, b_c, a_l, b_l = (jnp.flip(t, axis=1) for t in (a_c, b_c, a_l, b_l))
        h_c = linear_scan(a_c, b_c, jnp.zeros_like(a_c[:, 0]))
        h_l = linear_scan(a_l, b_l, h_c[:, -1])
        if d == 1:
            h_c, h_l = jnp.flip(h_c, axis=1), jnp.flip(h_l, axis=1)
        rec_lat.append(h_l)
        rec_ctx.append(h_c)
    rec_l = (rec_lat[0] + rec_lat[1]).astype(h_lat.dtype)
    y_lat = jnp.concatenate([fourier_grid(f_lat, w_four), jax.nn.gelu(g_lat) * rec_l], axis=-1) @ w_out
    if not with_ctx_out:
        return y_lat, None
    rec_c = (rec_ctx[0] + rec_ctx[1]).astype(h_ctx.dtype)
    y_ctx = jnp.concatenate([fourier_seq(f_ctx, w_four), jax.nn.gelu(g_ctx) * rec_c], axis=-1) @ w_out
    return y_lat, y_ctx


def swiglu(h, w_ffn_in, w_ffn_out):
    g, u = jnp.split(h @ w_ffn_in, 2, axis=-1)
    return (jax.nn.silu(g) * u) @ w_ffn_out


def setup_inputs(seed: int = 0) -> dict:
    key = jax.random.key(seed)
    ks = jax.random.split(key, 17)
    f32 = jnp.float32
    x = jax.random.normal(ks[0], (BATCH, SEQ, D_MODEL), f32)
    c = jax.random.normal(ks[1], (BATCH, D_MODEL), f32)
    ctx = jax.random.normal(ks[2], (BATCH, CTX_LEN, D_MODEL), f32)
    c_ctx = jax.random.normal(ks[3], (D_MODEL,), f32)
    w_ada = jax.random.normal(ks[4], (DEPTH, D_MODEL, N_MOD * D_MODEL), f32) * (0.5 * D_MODEL ** -0.5)
    b_ada = 0.02 * jax.random.normal(ks[5], (DEPTH, N_MOD * D_MODEL), f32)
    norm_g = 1.0 + 0.01 * jax.random.normal(ks[6], (DEPTH, N_NORMS, D_MODEL), f32)
    w_in = jax.random.normal(ks[7], (DEPTH, D_MODEL, IN_COLS), f32) * D_MODEL ** -0.5
    w_four = jax.random.normal(ks[8], (DEPTH, FOURIER_GROUPS, FOURIER_GROUP_DIM, FOURIER_GROUP_DIM), f32) * FOURIER_GROUP_DIM ** -0.5
    conv_w = jax.random.normal(ks[9], (DEPTH, CONV_WIDTH, LRU_WIDTH), f32) * CONV_WIDTH ** -0.5
    conv_b = 0.01 * jax.random.normal(ks[10], (DEPTH, LRU_WIDTH), f32)
    lru_w = jax.random.normal(ks[11], (DEPTH, 2, 2, LRU_HEADS, LRU_HEAD_DIM, LRU_HEAD_DIM), f32) * LRU_HEAD_DIM ** -0.5
    lru_b = 0.01 * jax.random.normal(ks[12], (DEPTH, 2, 2, LRU_WIDTH), f32)
    a_c = jax.random.uniform(ks[13], (DEPTH, 2, LRU_WIDTH), f32, minval=0.9, maxval=0.999)
    s = a_c ** (1.0 / LRU_C)
    lru_lam = jnp.log(s) - jnp.log1p(-s)
    w_out = jax.random.normal(ks[14], (DEPTH, MIX_WIDTH, D_MODEL), f32) * MIX_WIDTH ** -0.5
    w_ffn_in = jax.random.normal(ks[15], (DEPTH, D_MODEL, 2 * D_FF), f32) * D_MODEL ** -0.5
    w_ffn_out = jax.random.normal(ks[16], (DEPTH, D_FF, D_MODEL), f32) * D_FF ** -0.5
    return {"x": x, "c": c, "ctx": ctx, "c_ctx": c_ctx, "w_ada": w_ada, "b_ada": b_ada,
            "norm_g": norm_g, "w_in": w_in, "w_four": w_four, "conv_w": conv_w, "conv_b": conv_b,
            "lru_w": lru_w, "lru_b": lru_b, "lru_lam": lru_lam, "w_out": w_out,
            "w_ffn_in": w_ffn_in, "w_ffn_out": w_ffn_out}


def reference(x, c, ctx, c_ctx, w_ada, b_ada, norm_g, w_in, w_four, conv_w, conv_b,
              lru_w, lru_b, lru_lam, w_out, w_ffn_in, w_ffn_out):
    mod_lat = jnp.einsum("bd,ldm->lbm", jax.nn.silu(c), w_ada) + b_ada[:, None, :]
    mod_ctx = jnp.einsum("d,ldm->lm", jax.nn.silu(c_ctx), w_ada) + b_ada
    for layer in range(DEPTH):
        last = layer == DEPTH - 1
        ml = jnp.split(mod_lat[layer][:, None, :], N_MOD, axis=-1)
        mc = jnp.split(mod_ctx[layer][None, None, :], N_MOD, axis=-1)
        g = norm_g[layer]
        h_lat = modulate(rms_norm(x, g[0]), ml[0], ml[1])
        h_ctx = modulate(rms_norm(ctx, g[0]), mc[0], mc[1])
        y_lat, y_ctx = mixing_sublayer(h_lat, h_ctx, w_in[layer], w_four[layer], conv_w[layer], conv_b[layer],
                                       lru_w[layer], lru_b[layer], lru_lam[layer], w_out[layer],
                                       not last)
        x = x + ml[2] * rms_norm(y_lat, g[1])
        f_lat = swiglu(modulate(rms_norm(x, g[2]), ml[3], ml[4]), w_ffn_in[layer], w_ffn_out[layer])
        x = x + ml[5] * rms_norm(f_lat, g[3])
        if not last:
            ctx = ctx + mc[2] * rms_norm(y_ctx, g[1])
            f_ctx = swiglu(modulate(rms_norm(ctx, g[2]), mc[3], mc[4]), w_ffn_in[layer], w_ffn_out[layer])
            ctx = ctx + mc[5] * rms_norm(f_ctx, g[3])
    return x
```

```python
import contextlib
import numpy as np
import ml_dtypes
import concourse.bass as bass
import concourse.mybir as mybir
from concourse.bass_utils import run_bass_kernel_spmd

F32 = mybir.dt.float32
BF16 = mybir.dt.bfloat16
AF = mybir.ActivationFunctionType
ALU = mybir.AluOpType
AX = mybir.AxisListType
NPBF = ml_dtypes.bfloat16

D = 2048
SEQ = 8192
CTX = 256
NB = 4
DFF = 5632
EPS = 1e-6
NCORES = 8


class Buf:
    __slots__ = ("name", "w", "r")

    def __init__(self, name=""):
        self.name = name
        self.w = {}
        self.r = {}


class Sched:
    COMPUTE = ("pe", "act", "dve", "pool")

    def __init__(self, nc, n_dma_sems=24):
        self.nc = nc
        self.es = contextlib.ExitStack()
        self.streams = {e: [] for e in ("pe", "act", "dve", "pool", "sp")}
        self.sems = {}
        for e in self.COMPUTE:
            self.sems[e] = self.es.enter_context(nc.semaphore("sem_" + e))
        self.cnt = {e: 0 for e in self.COMPUTE}
        self.dsem = []
        for i in range(n_dma_sems):
            k = "d%d" % i
            self.sems[k] = self.es.enter_context(nc.semaphore("sem_" + k))
            self.dsem.append([k, 0])
        self.dnext = 0
        self.seen = {e: {} for e in self.streams}
        self.n_ins = 0
        self.n_wait = 0

    def op(self, eng, fn, reads=(), writes=(), dma=False, nosync_self=False, part=False):
        best = {}

        def need(d):
            for k, v in d.items():
                if nosync_self and k == eng:
                    continue
                if v > best.get(k, 0):
                    best[k] = v
        for b in reads:
            need(b.w)
        for b in writes:
            need(b.w)
            need(b.r)
        if dma:
            slot = self.dsem[self.dnext]
            self.dnext = (self.dnext + 1) % len(self.dsem)
            if slot[1] > best.get(slot[0], 0):
                best[slot[0]] = slot[1]
        waits = []
        seen = self.seen[eng]
        for k, v in best.items():
            if seen.get(k, 0) < v:
                seen[k] = v
                waits.append((k, v))
        if dma:
            slot[1] += 16
            tk, tv, inc = slot[0], slot[1], 16
        else:
            self.cnt[eng] += 1
            tk, tv, inc = eng, self.cnt[eng], 1
        self.streams[eng].append((waits, fn, tk, inc))
        self.n_ins += 1
        self.n_wait += len(waits)
        for b in reads:
            if b.r.get(tk, 0) < tv:
                b.r[tk] = tv
        for b in writes:
            if part:
                if b.w.get(tk, 0) < tv:
                    b.w[tk] = tv
            else:
                b.w = {tk: tv}
                b.r = {}

    def barrier(self):
        tgt = {e: self.cnt[e] for e in self.COMPUTE if self.cnt[e] > 0}
        for k_, tot in self.dsem:
            if tot > 0:
                tgt[k_] = tot
        for eng in self.streams:
            seen = self.seen[eng]
            waits = []
            for k_, v in tgt.items():
                if seen.get(k_, 0) < v:
                    seen[k_] = v
                    waits.append((k_, v))
            self.streams[eng].append((waits, None, None, 0))

    def finish_wait(self, eng, bufs):
        best = {}
        for b in bufs:
            for d in (b.w, b.r):
                for k, v in d.items():
                    if v > best.get(k, 0):
                        best[k] = v
        self.streams[eng].append((list(best.items()), None, None, 0))

    def emit(self):
        nc = self.nc
        sems = self.sems
        streams = self.streams
        with nc.Block() as block:
            def run(e, stream):
                for waits, fn, sk, inc in stream:
                    for k, v in waits:
                        e.wait_ge(sems[k], v)
                    if fn is not None:
                        fn(e).then_inc(sems[sk], inc)

            @block.sync
            def _(e):
                run(e, streams["sp"])

            @block.tensor
            def _(e):
                run(e, streams["pe"])

            @block.scalar
            def _(e):
                run(e, streams["act"])

            @block.vector
            def _(e):
                run(e, streams["dve"])

            @block.gpsimd
            def _(e):
                run(e, streams["pool"])


class T:
    __slots__ = ("t", "b")

    def __init__(self, t, name=""):
        self.t = t
        self.b = Buf(name)


class K:
    def __init__(self):
        self.nc = bass.Bass("TRN2", target_bir_lowering=False)
        self.S = Sched(self.nc)
        self.es = self.S.es
        self.outs = []
        self.nps = 0
        self.psl = []
        self.evq = 0
        self.phase_id = 0

    def sb(self, name, shape, dt):
        name = "%s_p%d" % (name, self.phase_id)
        return T(self.es.enter_context(self.nc.sbuf_tensor(name, list(shape), dt)), name)

    def din(self, name, shape, dt):
        return T(self.nc.dram_tensor(name, list(shape), dt, kind="ExternalInput").ap(), name)

    def dout(self, name, shape, dt):
        t = T(self.nc.dram_tensor(name, list(shape), dt, kind="ExternalOutput").ap(), name)
        self.outs.append(t)
        return t

    def dscr(self, name, shape, dt):
        return T(self.nc.dram_tensor(name, list(shape), dt, kind="Internal").ap(), name)

    @contextlib.contextmanager
    def phase(self):
        outer = self.es
        self.es = contextlib.ExitStack()
        self.phase_id += 1
        try:
            yield
        finally:
            self.S.barrier()
            self.es.close()
            self.es = outer

    def psum_pool(self, n=8):
        self.psl = [T(self.es.enter_context(self.nc.psum_tensor("ps%d" % i, [128, 512], F32)), "ps%d" % i)
                    for i in range(n)]

    def ps(self):
        p = self.psl[self.nps % len(self.psl)]
        self.nps += 1
        return p

    def op(self, eng, fn, reads=(), writes=(), **kw):
        self.S.op(eng, fn, [x.b for x in reads], [x.b for x in writes], **kw)

    def dma(self, q, out_ap, in_ap, reads=(), writes=(), part=False, slow=False):
        if callable(out_ap) or callable(in_ap):
            ev = lambda a: a() if callable(a) else a
            self.op(q, lambda e: e.dma_start(out=ev(out_ap), in_=ev(in_ap)), reads, writes, dma=True, part=part)
            return
        if slow:
            self.op(q, lambda e: e.dma_start(out=out_ap, in_=in_ap, allow_slow_non_contiguous=True), reads, writes,
                    dma=True, part=part)
        else:
            self.op(q, lambda e: e.dma_start(out=out_ap, in_=in_ap), reads, writes, dma=True, part=part)

    def mm(self, out_ap, lhsT, rhs, start, stop, reads, writes, part=None):
        self.op("pe", lambda e: e.matmul(out_ap, lhsT=lhsT, rhs=rhs, start=start, stop=stop),
                reads, writes, nosync_self=True, part=(not start) if part is None else part)

    def evac_eng(self):
        self.evq += 1
        return "act" if self.evq % 2 else "dve"

    def copy(self, eng, out_ap, in_ap, reads, writes, part=False, scale=None):
        if eng == "act":
            if scale is None:
                self.op("act", lambda e: e.activation(out=out_ap, in_=in_ap, func=AF.Copy), reads, writes, part=part)
            else:
                self.op("act", lambda e: e.activation(out=out_ap, in_=in_ap, func=AF.Copy, scale=scale),
                        reads, writes, part=part)
        else:
            if scale is None:
                self.op(eng, lambda e: e.tensor_copy(out=out_ap, in_=in_ap), reads, writes, part=part)
            else:
                self.op(eng, lambda e: e.tensor_scalar(out=out_ap, in0=in_ap, scalar1=float(scale), scalar2=None,
                                                       op0=ALU.mult), reads, writes, part=part)


    def act(self, out, in_, func, reads, writes, scale=None, bias=None, part=False):
        kw = {}
        if scale is not None:
            kw["scale"] = scale
        if bias is not None:
            kw["bias"] = bias
        self.op("act", lambda e: e.activation(out=out, in_=in_, func=func, **kw), reads, writes, part=part)

    def tt(self, eng, out, in0, in1, op, reads, writes, part=False):
        self.op(eng, lambda e: e.tensor_tensor(out=out, in0=in0, in1=in1, op=op), reads, writes, part=part)

    def ts(self, eng, out, in0, s1, s2, op0, op1, reads, writes, part=False):
        if s2 is None:
            self.op(eng, lambda e: e.tensor_scalar(out=out, in0=in0, scalar1=s1, scalar2=None, op0=op0),
                    reads, writes, part=part)
        else:
            self.op(eng, lambda e: e.tensor_scalar(out=out, in0=in0, scalar1=s1, scalar2=s2, op0=op0, op1=op1),
                    reads, writes, part=part)

    def stt(self, eng, out, in0, scalar, in1, op0, op1, reads, writes, part=False):
        self.op(eng, lambda e: e.scalar_tensor_tensor(out=out, in0=in0, scalar=scalar, in1=in1, op0=op0, op1=op1),
                reads, writes, part=part)

    def scan(self, out, d0, d1, init, reads, writes):
        self.op("dve", lambda e: e.tensor_tensor_scan(out=out, data0=d0, data1=d1, initial=init, op0=ALU.mult,
                                                      op1=ALU.add), reads, writes)

    def memset(self, eng, ap, val, writes, part=False):
        self.op(eng, lambda e: e.memset(ap, val), [], writes, part=part)

    def finish(self):
        self.S.finish_wait("sp", [t.b for t in self.outs])
        self.S.finish_wait("pool", [t.b for t in self.outs])
        self.S.emit()
        self.es.close()
        return self.nc


def run(nc, in_maps):
    res = run_bass_kernel_spmd(nc, in_maps, core_ids=list(range(len(in_maps))))
    return res.results


MCOLS = 2 * 6 * D // NCORES


def build_M():
    k = K()
    nc = k.nc
    csT = k.din("csT", [128, 16, 5], F32)
    w = k.din("w", [D, MCOLS], F32)
    b = k.din("b", [5, MCOLS], F32)
    o = k.dout("mod", [5, MCOLS], F32)
    k.psum_pool(4)
    cs = k.sb("cs", [128, 16, 5], F32)
    csb = k.sb("csb", [128, 16, 5], BF16)
    wt = k.sb("wt", [128, 16, MCOLS], BF16)
    bt = k.sb("bt", [5, MCOLS], F32)
    ot = k.sb("ot", [5, MCOLS], F32)
    k.dma("sp", cs.t[:], csT.t[:, :, :], writes=[cs])
    k.dma("sp", bt.t[:], b.t[:, :], writes=[bt])
    wk = [Buf() for _ in range(16)]
    for kc in range(16):
        k.S.op("pool", lambda e, kc=kc: e.dma_start(out=wt.t[:, kc, :], in_=w.t[kc * 128:(kc + 1) * 128, :]),
               [], [wk[kc]], dma=True)
    k.op("act", lambda e: e.activation(out=csb.t[:], in_=cs.t[:], func=AF.Silu), [cs], [csb])
    for nb in range(MCOLS // 512):
        p = k.ps()
        for kc in range(16):
            k.S.op("pe", lambda e, kc=kc, nb=nb, p=p: e.matmul(p.t[0:5, :], lhsT=csb.t[:, kc, :],
                                                                rhs=wt.t[:, kc, nb * 512:(nb + 1) * 512],
                                                                start=(kc == 0), stop=(kc == 15)),
                   [csb.b, wk[kc]], [p.b], nosync_self=True, part=(kc > 0))
        k.op("dve", lambda e, nb=nb, p=p: e.tensor_tensor(out=ot.t[0:5, nb * 512:(nb + 1) * 512], in0=p.t[0:5, :],
                                                          in1=bt.t[0:5, nb * 512:(nb + 1) * 512], op=ALU.add),
             [p, bt], [ot], part=True)
    k.dma("sp", o.t[:, :], ot.t[:], reads=[ot], writes=[o])
    return k.finish()


def rstd_from_ss(k, ss):
    k.op("dve", lambda e: e.tensor_scalar(out=ss.t[:, 1:2], in0=ss.t[:, 0:1], scalar1=1.0 / D, scalar2=EPS,
                                          op0=ALU.mult, op1=ALU.add), [ss], [ss])
    k.op("act", lambda e: e.activation(out=ss.t[:, 3:4], in_=ss.t[:, 1:2], func=AF.Sqrt), [ss], [ss])
    k.op("dve", lambda e: e.reciprocal(out=ss.t[:, 2:3], in_=ss.t[:, 3:4]), [ss], [ss])


def norm_to_hT(k, xsrc, s, m, G, shv, hTt, tb, idb, junk, xb, ss, tmp):
    k.op("act", lambda e: e.activation(out=junk.t[:], in_=xsrc.t[:], func=AF.Square, accum_out=ss.t[:, 0:1]),
         [xsrc], [junk, ss])
    rstd_from_ss(k, ss)
    k.op("act", lambda e: e.activation(out=xb.t[:], in_=xsrc.t[:], func=AF.Copy, scale=ss.t[:, 2:3]),
         [xsrc, ss], [xb])
    for half in range(2):
        p = k.ps()
        pb = p.t[:].bitcast(BF16)
        for kk in range(8):
            kc = half * 8 + kk
            k.op("pe", lambda e, kk=kk, kc=kc, pb=pb: e.transpose(out=pb[:, kk * 128:(kk + 1) * 128],
                                                                   in_=xb.t[:, kc * 128:(kc + 1) * 128],
                                                                   identity=idb.t[:]),
                 [xb, idb], [p], nosync_self=True, part=(kk > 0))
        pv = pb[:, 0:1024].rearrange("p (k t) -> p k t", k=8)
        gb = G[:, half * 8:(half + 1) * 8].unsqueeze(2).to_broadcast([128, 8, 128])
        sbv = shv[:, half * 8:(half + 1) * 8].unsqueeze(2).to_broadcast([128, 8, 128])
        tm = tmp[half]
        k.op("dve", lambda e, pv=pv, gb=gb, tm=tm: e.tensor_tensor(out=tm.t[:], in0=pv, in1=gb, op=ALU.mult),
             [p], [tm])
        k.op("dve", lambda e, sbv=sbv, tm=tm, half=half: e.tensor_tensor(
            out=hTt.t[:, half * 8:(half + 1) * 8, tb * 128:(tb + 1) * 128], in0=tm.t[:], in1=sbv, op=ALU.add),
             [tm], [hTt], part=True)


def build_A(NT_lat=32, n_ctx=1):
    k = K()
    NT = NT_lat + n_ctx
    x = k.din("x", [NT * 128, D], F32)
    vec = k.din("vec", [128, 5, 16], F32)
    w = k.din("w", [D, 3072], F32)
    idn = k.din("idn", [128, 128], F32)
    fT = k.dout("fT", [NT * 128, 1024], BF16)
    rgT = k.dout("rgT", [2048, NT * 128], BF16)
    k.psum_pool(8)
    wt = k.sb("wt", [128, 16, 3072], BF16)
    wk = [T(None) for _ in range(16)]
    vt = k.sb("vt", [128, 5, 16], F32)
    G = k.sb("G", [128, 2, 16], F32)
    idf = k.sb("idf", [128, 128], F32)
    idb = k.sb("idb", [128, 128], BF16)
    xt = [k.sb("xt%d" % i, [128, D], F32) for i in range(2)]
    junk = k.sb("junk", [128, D], BF16)
    xb = [k.sb("xb%d" % i, [128, D], BF16) for i in range(2)]
    ss = [k.sb("ss%d" % i, [128, 4], F32) for i in range(2)]
    tmp = [k.sb("tmp%d" % i, [128, 8, 128], F32) for i in range(2)]
    hT = [k.sb("hT%d" % i, [128, 16, 512], BF16) for i in range(2)]
    stg = [k.sb("stg%d" % i, [128, 512], BF16) for i in range(4)]
    k.dma("sp", vt.t[:], vec.t[:, :, :], writes=[vt])
    k.dma("sp", idf.t[:], idn.t[:, :], writes=[idf])
    for kc in range(16):
        k.dma("pool", wt.t[:, kc, :], w.t[kc * 128:(kc + 1) * 128, :], writes=[wk[kc]])
    k.op("dve", lambda e: e.tensor_copy(out=idb.t[:], in_=idf.t[:]), [idf], [idb])
    for m in range(2):
        k.op("dve", lambda e, m=m: e.scalar_tensor_tensor(out=G.t[:, m, :], in0=vt.t[:, 2 + 2 * m, :], scalar=1.0,
                                                          in1=vt.t[:, 0, :], op0=ALU.add, op1=ALU.mult),
             [vt], [G], part=True)
    blocks = []
    t0 = 0
    while t0 < NT_lat:
        n = min(4, NT_lat - t0)
        blocks.append((t0, n, 0))
        t0 += n
    for c in range(n_ctx):
        blocks.append((NT_lat + c, 1, 1))
    nst = 0
    for bi, (tile0, ntile, m) in enumerate(blocks):
        hTt = hT[bi % 2]
        ntok = ntile * 128
        tok0 = tile0 * 128
        for tb in range(ntile):
            ti = tile0 + tb
            s = ti % 2
            k.dma("sp", xt[s].t[:], x.t[ti * 128:(ti + 1) * 128, :], writes=[xt[s]])
            norm_to_hT(k, xt[s], s, m, G.t[:, m, :], vt.t[:, 1 + 2 * m, :], hTt, tb, idb, junk, xb[s], ss[s], tmp)
        for cc in range(16):
            p = k.ps()
            for kc in range(16):
                k.mm(p.t[:, 0:ntok], wt.t[:, kc, 1024 + cc * 128:1024 + (cc + 1) * 128], hTt.t[:, kc, 0:ntok],
                     kc == 0, kc == 15, [wk[kc], hTt], [p])
            st = stg[nst % 4]
            nst += 1
            k.copy(k.evac_eng(), st.t[:, 0:ntok], p.t[:, 0:ntok], [p], [st])
            k.dma("pool", rgT.t[cc * 128:(cc + 1) * 128, tok0:tok0 + ntok], st.t[:, 0:ntok], reads=[st],
                  writes=[rgT], part=True)
        for tb in range(ntile):
            for nb in range(2):
                p = k.ps()
                for kc in range(16):
                    k.mm(p.t[:, :], hTt.t[:, kc, tb * 128:(tb + 1) * 128], wt.t[:, kc, nb * 512:(nb + 1) * 512],
                         kc == 0, kc == 15, [wk[kc], hTt], [p])
                st = stg[nst % 4]
                nst += 1
                k.copy(k.evac_eng(), st.t[:, :], p.t[:, :], [p], [st])
                k.dma("pool", fT.t[tok0 + tb * 128:tok0 + (tb + 1) * 128, nb * 512:(nb + 1) * 512], st.t[:, :],
                      reads=[st], writes=[fT], part=True)
    return k.finish()


def build_C(NT_lat=32, n_ctx=1, stage=9, ctxstage=9):
    k = K()
    NT = NT_lat + n_ctx
    NTOK = NT * 128
    mixT = k.din("mixT", [2048, NTOK], BF16)
    x = k.din("x", [NTOK, D], F32)
    vec = k.din("vec", [128, 5, 16], F32)
    rows = k.din("rows", [6, 128, D], F32)
    w_out = k.din("w_out", [2048, 2048], F32)
    w_in = k.din("w_ffn_in", [2048, 2 * DFF], F32)
    w_o2 = k.din("w_ffn_out", [DFF, 2048], F32)
    idn = k.din("idn", [128, 128], F32)
    xo = k.dout("xo", [NTOK, D], F32)
    s_out = k.dscr("s_out", [2048, 2048], BF16)
    s_in = k.dscr("s_in", [2048, 2 * DFF], BF16)
    s_o2 = k.dscr("s_o2", [DFF, 2048], BF16)
    k.psum_pool(8)
    wb = [k.sb("wb%d" % i, [128, 16, 512], BF16) for i in range(3)]
    aT = k.sb("aT", [128, 44, 512], BF16)
    mT = k.sb("mT", [128, 16, 512], BF16)
    hT = mT
    yb = [k.sb("yb%d" % i, [128, D], F32) for i in range(4)]
    xt = [k.sb("xt%d" % i, [128, D], F32) for i in range(2)]
    GG = [k.sb("GG%d" % i, [128, D], F32) for i in range(2)]
    junk = k.sb("junk", [128, D], BF16)
    xb = k.sb("xb", [128, D], BF16)
    tmp = [k.sb("tmp%d" % i, [128, 8, 128], F32) for i in range(2)]
    sgt = [k.sb("sg%d" % i, [128, 512], F32) for i in range(2)]
    ssq = [k.sb("ssq%d" % i, [128, 4], F32) for i in range(4)]
    ss = [k.sb("ss%d" % i, [128, 4], F32) for i in range(2)]
    vt = k.sb("vt", [128, 5, 16], F32)
    G = k.sb("G", [128, 2, 16], F32)
    idf = k.sb("idf", [128, 128], F32)
    idb = k.sb("idb", [128, 128], BF16)
    c_out = [T(None) for _ in range(16)]
    c_in = [T(None) for _ in range(16)]
    c_o2 = [T(None) for _ in range(44)]
    for kc in range(16):
        k.dma("pool", s_out.t[kc * 128:(kc + 1) * 128, :], w_out.t[kc * 128:(kc + 1) * 128, :], writes=[c_out[kc]])
    k.dma("sp", vt.t[:], vec.t[:, :, :], writes=[vt])
    k.dma("sp", idf.t[:], idn.t[:, :], writes=[idf])
    k.op("dve", lambda e: e.tensor_copy(out=idb.t[:], in_=idf.t[:]), [idf], [idb])
    for m in range(2):
        k.op("dve", lambda e, m=m: e.scalar_tensor_tensor(out=G.t[:, m, :], in0=vt.t[:, 2 + 2 * m, :], scalar=1.0,
                                                          in1=vt.t[:, 0, :], op0=ALU.add, op1=ALU.mult),
             [vt], [G], part=True)

    def load_GG(m):
        for j in range(2):
            k.dma("sp", GG[j].t[:], rows.t[1 + 2 * j + 4 * m if False else (1 + 2 * j if m == 0 else 4 + j), :, :],
                  writes=[GG[j]])
            k.dma("sp", xt[1].t[:], rows.t[2 * j, :, :], writes=[xt[1]])
            k.op("dve", lambda e, j=j: e.tensor_tensor(out=GG[j].t[:], in0=GG[j].t[:], in1=xt[1].t[:], op=ALU.mult),
                 [GG[j], xt[1]], [GG[j]])
    load_GG(0)
    if stage >= 2:
        for kc in range(16):
            k.dma("pool", s_in.t[kc * 128:(kc + 1) * 128, :], w_in.t[kc * 128:(kc + 1) * 128, :], writes=[c_in[kc]])
        for kc in range(44):
            k.dma("pool", s_o2.t[kc * 128:(kc + 1) * 128, :], w_o2.t[kc * 128:(kc + 1) * 128, :], writes=[c_o2[kc]])
    nwb = [0]

    def load_piece(src_ap, nk, deps):
        w = wb[nwb[0] % 3]
        nwb[0] += 1
        k0 = 0
        first = True
        while k0 < nk:
            k1 = min(nk, k0 + 4)
            k.dma("sp", w.t[:, k0:k1, :], src_ap[:, k0:k1, :], reads=deps, writes=[w], part=not first)
            first = False
            k0 = k1
        return w

    blocks = []
    t0 = 0
    while t0 < NT_lat:
        n = min(4, NT_lat - t0)
        blocks.append((t0, n, 0))
        t0 += n
    for c in range(n_ctx):
        blocks.append((NT_lat + c, 1, 1))
    cur_m = 0
    stage0 = stage
    for bi, (tile0, ntile, m) in enumerate(blocks):
        ntok = ntile * 128
        tok0 = tile0 * 128
        stage = ctxstage if m == 1 else stage0
        if m != cur_m and stage0 != 8:
            load_GG(m)
            cur_m = m
        for kc in range(16):
            k.dma("sp", mT.t[:, kc, 0:ntok], mixT.t[kc * 128:(kc + 1) * 128, tok0:tok0 + ntok], writes=[mT],
                  part=(kc > 0))
        if stage < 1:
            continue
        for nb in range(4):
            w = load_piece(s_out.t[:, nb * 512:(nb + 1) * 512].rearrange("(k p) c -> p k c", p=128), 16, c_out)
            for tb in range(ntile):
                p = k.ps()
                for kc in range(16):
                    k.mm(p.t[:, :], mT.t[:, kc, tb * 128:(tb + 1) * 128], w.t[:, kc, :], kc == 0, kc == 15,
                         [mT, w], [p])
                k.op("dve", lambda e, p=p, tb=tb, nb=nb: e.tensor_copy(out=yb[tb].t[:, nb * 512:(nb + 1) * 512],
                                                                      in_=p.t[:, :]), [p], [yb[tb]], part=True)
        if stage < 2:
            continue
        for tb in range(ntile):
            ti = tile0 + tb
            s = ti % 2
            sst = ss[s]
            k.op("act", lambda e, tb=tb, sst=sst: e.activation(out=junk.t[:], in_=yb[tb].t[:], func=AF.Square,
                                                              accum_out=sst.t[:, 0:1]), [yb[tb]], [junk, sst])
            rstd_from_ss(k, sst)
            k.dma("sp", xt[s].t[:], x.t[ti * 128:(ti + 1) * 128, :], writes=[xt[s]])
            k.op("dve", lambda e, tb=tb, sst=sst: e.scalar_tensor_tensor(out=yb[tb].t[:], in0=yb[tb].t[:],
                                                                        scalar=sst.t[:, 2:3], in1=GG[0].t[:],
                                                                        op0=ALU.mult, op1=ALU.mult),
                 [yb[tb], sst, GG[0]], [yb[tb]])
            k.op("pool", lambda e, tb=tb, s=s: e.tensor_tensor(out=xt[s].t[:], in0=yb[tb].t[:], in1=xt[s].t[:],
                                                               op=ALU.add), [yb[tb], xt[s]], [xt[s]])
            k.dma("sp", xo.t[ti * 128:(ti + 1) * 128, :], xt[s].t[:], reads=[xt[s]], writes=[xo], part=True)
            norm_to_hT(k, xt[s], s, m, G.t[:, m, :], vt.t[:, 1 + 2 * m, :], hT, tb, idb, junk, xb, sst, tmp)
        if stage < 3:
            continue
        for mg in range(11):
            wg = load_piece(s_in.t[:, mg * 512:(mg + 1) * 512].rearrange("(k p) c -> p k c", p=128), 16, c_in)
            wu = load_piece(s_in.t[:, DFF + mg * 512:DFF + (mg + 1) * 512].rearrange("(k p) c -> p k c", p=128),
                            16, c_in)
            for j in range(4):
                mi = mg * 4 + j
                pg = k.ps()
                for kc in range(16):
                    k.mm(pg.t[:, 0:ntok], wg.t[:, kc, j * 128:(j + 1) * 128], hT.t[:, kc, 0:ntok], kc == 0, kc == 15,
                         [wg, hT], [pg])
                pu = k.ps()
                for kc in range(16):
                    k.mm(pu.t[:, 0:ntok], wu.t[:, kc, j * 128:(j + 1) * 128], hT.t[:, kc, 0:ntok], kc == 0, kc == 15,
                         [wu, hT], [pu])
                sg = sgt[mi % 2]
                k.op("act", lambda e, pg=pg, sg=sg, ntok=ntok: e.activation(out=sg.t[:, 0:ntok], in_=pg.t[:, 0:ntok],
                                                                func=AF.Silu), [pg], [sg])
                k.op("dve", lambda e, pu=pu, sg=sg, mi=mi, ntok=ntok: e.tensor_tensor(out=aT.t[:, mi, 0:ntok],
                                                                           in0=sg.t[:, 0:ntok], in1=pu.t[:, 0:ntok],
                                                                           op=ALU.mult), [sg, pu], [aT], part=True)
        if stage < 4:
            continue
        for nb in range(4):
            banks = [k.ps() for _ in range(ntile)]
            for kh in range(4):
                w = load_piece(s_o2.t[kh * 11 * 128:(kh + 1) * 11 * 128, nb * 512:(nb + 1) * 512]
                               .rearrange("(k p) c -> p k c", p=128), 11, c_o2)
                for tb in range(ntile):
                    for kk in range(11):
                        k.mm(banks[tb].t[:, :], aT.t[:, kh * 11 + kk, tb * 128:(tb + 1) * 128], w.t[:, kk, :],
                             kh == 0 and kk == 0, kh == 3 and kk == 10, [aT, w], [banks[tb]])
            for tb in range(ntile):
                p = banks[tb]
                k.op("dve", lambda e, p=p, tb=tb, nb=nb: e.tensor_copy(out=yb[tb].t[:, nb * 512:(nb + 1) * 512],
                                                                      in_=p.t[:, :]), [p], [yb[tb]], part=True)
        if stage < 5:
            continue
        for tb in range(ntile):
            ti = tile0 + tb
            s = ti % 2
            sst = ss[s]
            k.op("act", lambda e, tb=tb, sst=sst: e.activation(out=junk.t[:], in_=yb[tb].t[:], func=AF.Square,
                                                              accum_out=sst.t[:, 0:1]), [yb[tb]], [junk, sst])
            rstd_from_ss(k, sst)
            k.dma("sp", xt[s].t[:], xo.t[ti * 128:(ti + 1) * 128, :], reads=[xo], writes=[xt[s]])
            k.op("dve", lambda e, tb=tb, sst=sst: e.scalar_tensor_tensor(out=yb[tb].t[:], in0=yb[tb].t[:],
                                                                        scalar=sst.t[:, 2:3], in1=GG[1].t[:],
                                                                        op0=ALU.mult, op1=ALU.mult),
                 [yb[tb], sst, GG[1]], [yb[tb]])
            k.op("pool", lambda e, tb=tb, s=s: e.tensor_tensor(out=xt[s].t[:], in0=yb[tb].t[:], in1=xt[s].t[:],
                                                               op=ALU.add), [yb[tb], xt[s]], [xt[s]])
            k.dma("sp", xo.t[ti * 128:(ti + 1) * 128, :], xt[s].t[:], reads=[xt[s]], writes=[xo], part=True)
    return k.finish()


NTB = CTX + SEQ
NV = 48


class V:
    __slots__ = ("t", "b")

    def __init__(self, ap, parent):
        self.t = ap
        self.b = parent.b


def build_B():
    k = K()
    X0 = k.din("X0", [4, 128, 64, 128], BF16)
    fctx = k.din("fctx", [128, 2, 512], BF16)
    rT = k.din("rT", [128, 4, NTB], BF16)
    gT = k.din("gT", [128, 4, NTB], BF16)
    wf = k.din("wf", [128, 2, 2, 256], F32)
    lw = k.din("lw", [128, 16, 256], F32)
    cs64 = k.din("cs64", [128, 128], F32)
    rs = k.din("rs", [128, 2, 256], F32)
    c256 = k.din("c256", [128, 2, 512], F32)
    vecs = k.din("vecs", [128, NV], F32)
    mixT = k.dout("mixT", [1024, NTB], BF16)
    hf_scr = k.dscr("hf_scr", [128, 4, NTB], F32)
    k.psum_pool(8)
    idn = k.din("idn", [128, 128], F32)
    idf = k.sb("idf", [128, 128], F32)
    idb = k.sb("idb", [128, 128], BF16)
    k.dma("sp", idf.t[:], idn.t[:, :], writes=[idf])
    k.copy("dve", idb.t[:], idf.t[:], [idf], [idb])
    emit_B(k, X0, fctx, rT, gT, wf, lw, cs64, rs, c256, vecs, mixT, hf_scr, 0, 512, idb)
    return k.finish()


def emit_B(k, X0, fctx, rT, gT, wf, lw, cs64, rs, c256, vecs, mixT, hf_scr, ROWF, ROWL, idb):
    vt = k.sb("vt", [128, NV], F32)
    ost = [k.sb("ost%d" % i, [128, 512], BF16) for i in range(4)]
    k.dma("sp", vt.t[:], vecs.t[:, :], writes=[vt])
    nost = [0]

    def next_ost():
        o = ost[nost[0] % 4]
        nost[0] += 1
        return o

    with k.phase():
        wfb = k.sb("wfb", [128, 2, 2, 256], BF16)
        cs64b = k.sb("cs64b", [128, 128], BF16)
        rsb = k.sb("rsb", [128, 2, 256], BF16)
        c256b = k.sb("c256b", [128, 2, 512], BF16)
        MP = k.sb("MP", [128, 2, 2, 256], BF16)
        MQ = k.sb("MQ", [128, 2, 2, 256], BF16)
        X0c = [k.sb("X0c%d" % i, [128, 64, 128], BF16) for i in range(2)]
        X1c = k.sb("X1c", [128, 128, 128], BF16)
        PQT = k.sb("PQT", [128, 2, 2, SEQ], BF16)
        PQc = k.sb("PQc", [128, 4, 512], BF16)
        fcb = k.sb("fcb", [128, 2, 512], BF16)
        for dst, src in ((wfb, wf), (cs64b, cs64), (rsb, rs), (c256b, c256)):
            k.dma("pool", dst.t[:], src.t[:], writes=[dst])
        k.dma("sp", fcb.t[:], fctx.t[:, :, :], writes=[fcb])
        for g in range(2):
            for ic in range(2):
                p = k.ps()
                for kc in range(2):
                    k.mm(p.t[:, 0:256], c256b.t[:, kc, ic * 128:(ic + 1) * 128], wfb.t[:, g, kc, :], kc == 0, kc == 1,
                         [c256b, wfb], [p], part=(kc > 0))
                for kc in range(2):
                    k.mm(p.t[:, 256:512], c256b.t[:, kc, 256 + ic * 128:256 + (ic + 1) * 128], wfb.t[:, g, kc, :],
                         kc == 0, kc == 1, [c256b, wfb], [p], part=True)
                k.copy("dve", MP.t[:, g, ic, :], p.t[:, 0:256], [p], [MP], part=True)
                k.copy("dve", MQ.t[:, g, ic, :], p.t[:, 256:512], [p], [MQ], part=True, scale=-1.0)
        for ic in range(4):
            p = k.ps()
            for lc in range(2):
                k.mm(p.t[:, :], fcb.t[:, lc, ic * 128:(ic + 1) * 128], c256b.t[:, lc, :], lc == 0, lc == 1,
                     [fcb, c256b], [p])
            k.copy(k.evac_eng(), PQc.t[:, ic, :], p.t[:, :], [p], [PQc], part=True, scale=1.0 / 256.0)
        for g in range(2):
            for jc in range(2):
                p = k.ps()
                n = 0
                for icl in range(2):
                    for pq in range(2):
                        M = MP if pq == 0 else MQ
                        k.mm(p.t[:, 0:256], M.t[:, g, icl, jc * 128:(jc + 1) * 128],
                             PQc.t[:, g * 2 + icl, pq * 256:(pq + 1) * 256], n == 0, n == 3, [M, PQc], [p])
                        n += 1
                o = next_ost()
                k.copy(k.evac_eng(), o.t[:, 0:256], p.t[:, 0:256], [p], [o])
                k.dma("pool", mixT.t[ROWF + g * 256 + jc * 128:ROWF + g * 256 + (jc + 1) * 128, 0:256], o.t[:, 0:256], reads=[o],
                      writes=[mixT], part=True)
        NORM = float((128.0 * 64.0 * 256.0) ** -0.5)
        for ic in range(4):
            g, icl = ic // 2, ic % 2
            xc = X0c[ic % 2]
            k.dma("sp", xc.t[:], X0.t[ic, :, :, :], writes=[xc])
            for i4 in range(32):
                p = k.ps()
                for ii in range(4):
                    i = i4 * 4 + ii
                    k.mm(p.t[0:64, ii * 128:(ii + 1) * 128], xc.t[0:64, :, i], cs64b.t[0:64, :], True, True,
                         [xc, cs64b], [p], part=(ii > 0))
                    k.mm(p.t[64:128, ii * 128:(ii + 1) * 128], xc.t[64:128, :, i], cs64b.t[64:128, :], True, True,
                         [xc, cs64b], [p], part=True)
                k.copy(k.evac_eng(), X1c.t[:, i4 * 4:(i4 + 1) * 4, :],
                       p.t[:, :].rearrange("p (a b) -> p a b", a=4), [p], [X1c], part=(i4 > 0))
            pqv = PQT.t[:, icl, :, :].rearrange("p a (r c) -> p a r c", c=64)
            for kc2 in range(32):
                p = k.ps()
                for q in range(2):
                    kc = kc2 * 2 + q
                    k.mm(p.t[:, q * 256:(q + 1) * 256], X1c.t[:, :, kc], rsb.t[:, 0, :], True, False, [X1c, rsb], [p],
                         part=(q > 0))
                    k.mm(p.t[:, q * 256:(q + 1) * 256], X1c.t[:, :, 64 + kc], rsb.t[:, 1, :], False, True, [X1c, rsb], [p],
                         part=True)
                for q in range(2):
                    kc = kc2 * 2 + q
                    k.copy(k.evac_eng(), pqv[:, :, :, kc], p.t[:, q * 256:(q + 1) * 256].rearrange("p (a r) -> p a r", a=2),
                           [p], [PQT], part=True, scale=NORM)
            if icl == 1:
                for tb in range(16):
                    for jc in range(2):
                        p = k.ps()
                        n = 0
                        for il in range(2):
                            for pq in range(2):
                                M = MP if pq == 0 else MQ
                                k.mm(p.t[:, :], M.t[:, g, il, jc * 128:(jc + 1) * 128],
                                     PQT.t[:, il, pq, tb * 512:(tb + 1) * 512], n == 0, n == 3, [M, PQT], [p])
                                n += 1
                        o = next_ost()
                        k.copy(k.evac_eng(), o.t[:, :], p.t[:, :], [p], [o])
                        k.dma("pool", mixT.t[ROWF + g * 256 + jc * 128:ROWF + g * 256 + (jc + 1) * 128,
                                           CTX + tb * 512:CTX + (tb + 1) * 512], o.t[:, :], reads=[o], writes=[mixT],
                              part=True)
    with k.phase():
        lwb = k.sb("lwb", [128, 16, 256], BF16)
        csv = k.sb("csv", [128, 16], F32)
        k.dma("pool", lwb.t[:], lw.t[:], writes=[lwb])
        TC = 512
        mk = lambda nm, shape, dt: [[k.sb("%s%d_%d" % (nm, par, i), shape, dt) for i in range(4)] for par in range(2)]
        R = mk("R", [128, TC + 4], BF16)
        u = mk("u", [128, TC], F32)
        ub = mk("ub", [128, TC], BF16)
        rg = mk("rg", [128, TC], F32)
        ig = mk("ig", [128, TC], F32)
        av = mk("av", [128, TC], F32)
        a2 = mk("a2", [128, TC], F32)
        hfl = mk("hfl", [128, TC], F32)
        gch = mk("gch", [128, TC], BF16)
        gl = mk("gl", [128, TC], F32)
        state = k.sb("state", [128, 8], F32)
        ONE = vt.t[:, 44:45]
        k.act(csv.t[:, 0:8], vt.t[:, 36:44], AF.Exp, [vt], [csv], scale=-1.0)
        k.op("dve", lambda e: e.tensor_scalar(out=csv.t[:, 0:8], in0=csv.t[:, 0:8], scalar1=1.0, scalar2=None,
                                              op0=ALU.add), [csv], [csv])
        k.act(csv.t[:, 0:8], csv.t[:, 0:8], AF.Ln, [csv], [csv])
        k.op("dve", lambda e: e.tensor_scalar(out=csv.t[:, 8:16], in0=csv.t[:, 0:8], scalar1=-16.0, scalar2=None,
                                              op0=ALU.mult), [csv], [csv])
        k.op("dve", lambda e: e.tensor_scalar(out=csv.t[:, 0:8], in0=csv.t[:, 0:8], scalar1=-8.0, scalar2=None,
                                              op0=ALU.mult), [csv], [csv])
        k.memset("dve", state.t[:], 0.0, [state])
        chunks = [(0, CTX, True, True)] + [(CTX + j * TC, TC, j == 0, j == SEQ // TC - 1) for j in range(SEQ // TC)]

        def conv_chunk(par, chunk, d):
            t0, Tn, first, last = chunk
            for cc in range(4):
                Rt = R[par][cc]
                lo = t0 if first else t0 - 2
                hi = t0 + Tn if last else t0 + Tn + 1
                k.memset("dve", Rt.t[:, 0:2], 0.0, [Rt])
                k.memset("dve", Rt.t[:, Tn + 2:Tn + 4], 0.0, [Rt], part=True)
                k.dma("sp", Rt.t[:, lo - (t0 - 2):hi - (t0 - 2)], rT.t[:, cc, lo:hi], writes=[Rt], part=True)
                if d == 1:
                    k.dma("sp", hfl[par][cc].t[:, 0:Tn], hf_scr.t[:, cc, t0:t0 + Tn], reads=[hf_scr],
                          writes=[hfl[par][cc]])
                    k.dma("sp", gch[par][cc].t[:, 0:Tn], gT.t[:, cc, t0:t0 + Tn], writes=[gch[par][cc]])
            for cc in range(4):
                Rt, uu = R[par][cc], u[par][cc]
                k.ts("dve", uu.t[:, 0:Tn], Rt.t[:, 0:Tn], vt.t[:, cc:cc + 1], vt.t[:, 16 + cc:17 + cc], ALU.mult,
                     ALU.add, [Rt, vt], [uu])
                for tap in range(1, 4):
                    k.stt("dve", uu.t[:, 0:Tn], Rt.t[:, tap:tap + Tn], vt.t[:, tap * 4 + cc:tap * 4 + cc + 1],
                          uu.t[:, 0:Tn], ALU.mult, ALU.add, [Rt, vt, uu], [uu])
                k.copy("act", ub[par][cc].t[:, 0:Tn], uu.t[:, 0:Tn], [uu], [ub[par][cc]])

        def act_chain(par, chunk, d):
            t0, Tn, first, last = chunk
            for cc in range(4):
                head, jc = cc // 2, cc % 2
                for gate, dst in ((0, rg), (1, ig)):
                    p = k.ps()
                    for kc in range(2):
                        wi = ((d * 2 + gate) * 2 + head) * 2 + kc
                        k.mm(p.t[:, 0:Tn], lwb.t[:, wi, jc * 128:(jc + 1) * 128], ub[par][head * 2 + kc].t[:, 0:Tn],
                             kc == 0, kc == 1, [lwb, ub[par][head * 2 + kc]], [p])
                    bcol = 20 + (d * 2 + gate) * 4 + cc
                    k.act(dst[par][cc].t[:, 0:Tn], p.t[:, 0:Tn], AF.Sigmoid, [p, vt], [dst[par][cc]],
                          bias=vt.t[:, bcol:bcol + 1])
            for cc in range(4):
                k.act(av[par][cc].t[:, 0:Tn], rg[par][cc].t[:, 0:Tn], AF.Exp, [rg[par][cc], csv], [av[par][cc]],
                      scale=csv.t[:, d * 4 + cc:d * 4 + cc + 1])
                k.act(a2[par][cc].t[:, 0:Tn], rg[par][cc].t[:, 0:Tn], AF.Exp, [rg[par][cc], csv], [a2[par][cc]],
                      scale=csv.t[:, 8 + d * 4 + cc:8 + d * 4 + cc + 1])
            for cc in range(4):
                k.act(a2[par][cc].t[:, 0:Tn], a2[par][cc].t[:, 0:Tn], AF.Sqrt, [a2[par][cc], vt], [a2[par][cc]],
                      scale=-1.0, bias=ONE)
            if d == 1:
                for cc in range(4):
                    k.act(gl[par][cc].t[:, 0:Tn], gch[par][cc].t[:, 0:Tn], AF.Gelu_apprx_tanh, [gch[par][cc]],
                          [gl[par][cc]])

        def dve_part(par, chunk, d):
            t0, Tn, first, last = chunk
            for cc in range(4):
                igt, rgt, avt = ig[par][cc], rg[par][cc], av[par][cc]
                k.tt("dve", igt.t[:, 0:Tn], igt.t[:, 0:Tn], u[par][cc].t[:, 0:Tn], ALU.mult, [igt, u[par][cc]], [igt])
                k.tt("dve", igt.t[:, 0:Tn], igt.t[:, 0:Tn], a2[par][cc].t[:, 0:Tn], ALU.mult, [igt, a2[par][cc]], [igt])
                sc = state.t[:, d * 4 + cc:d * 4 + cc + 1]
                if d == 0:
                    k.scan(rgt.t[:, 0:Tn], avt.t[:, 0:Tn], igt.t[:, 0:Tn], sc, [avt, igt, state], [rgt])
                    k.copy("dve", sc, rgt.t[:, Tn - 1:Tn], [rgt], [state])
                    k.dma("pool", hf_scr.t[:, cc, t0:t0 + Tn], rgt.t[:, 0:Tn], reads=[rgt], writes=[hf_scr], part=True)
                else:
                    k.scan(rgt.t[:, 0:Tn][:, ::-1], avt.t[:, 0:Tn][:, ::-1], igt.t[:, 0:Tn][:, ::-1], sc,
                           [avt, igt, state], [rgt])
                    k.copy("dve", sc, rgt.t[:, 0:1], [rgt], [state])
                    ht = hfl[par][cc]
                    k.tt("dve", ht.t[:, 0:Tn], ht.t[:, 0:Tn], rgt.t[:, 0:Tn], ALU.add, [ht, rgt], [ht])
                    o = next_ost()
                    k.tt("dve", o.t[:, 0:Tn], ht.t[:, 0:Tn], gl[par][cc].t[:, 0:Tn], ALU.mult, [ht, gl[par][cc]], [o])
                    k.dma("pool", mixT.t[ROWL + cc * 128:ROWL + (cc + 1) * 128, t0:t0 + Tn], o.t[:, 0:Tn], reads=[o],
                          writes=[mixT], part=True)

        for d, order in ((0, chunks), (1, [chunks[0]] + chunks[:0:-1])):
            conv_chunk(0, order[0], d)
            for n, chunk in enumerate(order):
                par = n % 2
                act_chain(par, chunk, d)
                if n + 1 < len(order):
                    conv_chunk(1 - par, order[n + 1], d)
                dve_part(par, chunk, d)
def _pp(v):
    return np.ascontiguousarray(np.asarray(v, np.float32).reshape(16, 128).T)


def _b_consts():
    c = np.arange(64)
    ang = 2.0 * np.pi * np.outer(c, c) / 64.0
    cs = np.concatenate([np.cos(ang), np.sin(ang)], axis=1)
    cs64 = np.concatenate([cs, cs], axis=0)
    p = np.arange(128)
    r = 2 * (p % 64) + p // 64
    ang = 2.0 * np.pi * np.outer(r, np.arange(128)) / 128.0
    rs1 = np.concatenate([np.cos(ang), np.sin(ang)], axis=1)
    rs2 = np.concatenate([-np.sin(ang), np.cos(ang)], axis=1)
    rs = np.stack([rs1, rs2], axis=1)
    n = np.arange(256)
    ang = 2.0 * np.pi * np.outer(n, n) / 256.0
    cc = np.concatenate([np.cos(ang), np.sin(ang)], axis=1)
    c256 = cc.reshape(2, 128, 512).transpose(1, 0, 2)
    f = lambda a: np.ascontiguousarray(a, dtype=np.float32)
    return {"cs64": f(cs64), "rs": f(rs), "c256": f(c256), "idn": np.eye(128, dtype=np.float32)}


def _b_weights(hf, w_four_l, conv_w_l, conv_b_l, lru_w_l, lru_b_l, lru_lam_l):
    wf = np.asarray(w_four_l, np.float32)[2 * hf:2 * hf + 2]
    wf = wf.reshape(2, 2, 128, 256).transpose(2, 0, 1, 3)
    lw = np.asarray(lru_w_l, np.float32)[:, :, 2 * hf:2 * hf + 2]
    lw = lw.reshape(2, 2, 2, 2, 128, 256).transpose(4, 0, 1, 2, 3, 5).reshape(128, 16, 256)
    sl = slice(hf * 512, (hf + 1) * 512)
    vec = np.zeros((128, NV), np.float32)
    col = lambda v: np.asarray(v, np.float32)[sl].reshape(4, 128).T
    for tap in range(4):
        vec[:, tap * 4:tap * 4 + 4] = col(conv_w_l[tap])
    vec[:, 16:20] = col(conv_b_l)
    for d in range(2):
        for g in range(2):
            vec[:, 20 + (d * 2 + g) * 4:24 + (d * 2 + g) * 4] = col(lru_b_l[d, g])
        vec[:, 36 + d * 4:40 + d * 4] = col(lru_lam_l[d])
    vec[:, 44] = 1.0
    return {"wf": np.ascontiguousarray(wf), "lw": np.ascontiguousarray(lw), "vecs": vec}


def _b_acts(f_lat, f_ctx, r_full, g_full):
    X0 = f_lat.reshape(64, 2, 64, 4, 128).transpose(3, 1, 2, 0, 4).reshape(4, 128, 64, 128)
    fc = f_ctx.reshape(2, 128, 512).transpose(1, 0, 2)
    rT = r_full.reshape(4, 128, NTB).transpose(1, 0, 2)
    gT = g_full.reshape(4, 128, NTB).transpose(1, 0, 2)
    c = np.ascontiguousarray
    return {"X0": c(X0), "fctx": c(fc), "rT": c(rT), "gT": c(gT)}


def _mod_rows(c, c_ctx, w_ada, b_ada):
    cs = np.concatenate([np.asarray(c, np.float32), np.asarray(c_ctx, np.float32)[None]], axis=0)
    csT = np.ascontiguousarray(cs.T.reshape(16, 128, 5).transpose(1, 0, 2))
    wcat = np.concatenate([np.asarray(w_ada[l], np.float32) for l in range(2)], axis=1)
    bcat = np.concatenate([np.asarray(b_ada[l], np.float32) for l in range(2)], axis=0)
    nc = build_M()
    maps = []
    for j in range(NCORES):
        sl = slice(j * MCOLS, (j + 1) * MCOLS)
        maps.append({"csT": csT, "w": np.ascontiguousarray(wcat[:, sl]),
                     "b": np.ascontiguousarray(np.broadcast_to(bcat[sl], (5, MCOLS)))})
    res = run(nc, maps)
    return np.concatenate([r["mod"] for r in res], axis=1)


def kernel_unfused(x, c, ctx, c_ctx, w_ada, b_ada, norm_g, w_in, w_four, conv_w, conv_b, lru_w, lru_b, lru_lam,
           w_out, w_ffn_in, w_ffn_out):
    f32 = lambda a: np.asarray(a, np.float32)
    x, ctx, norm_g = f32(x), f32(ctx), f32(norm_g)
    w_in, w_out, w_ffn_in, w_ffn_out = f32(w_in), f32(w_out), f32(w_ffn_in), f32(w_ffn_out)
    mod = _mod_rows(c, c_ctx, w_ada, b_ada)
    idn = np.eye(128, dtype=np.float32)
    consts = _b_consts()
    cat = np.concatenate
    ncA = build_A(32, 1)
    ncB = build_B()
    xs = []
    for core in range(NCORES):
        b, h = core // 2, core % 2
        xs.append(np.ascontiguousarray(cat([x[b, h * 4096:(h + 1) * 4096], ctx[b, h * 128:(h + 1) * 128]], axis=0)))
    for layer in range(2):
        last = layer == 1
        mv = lambda row, j: mod[row, layer * 6 * D + j * D:layer * 6 * D + (j + 1) * D]
        g = norm_g[layer]
        maps = []
        for core in range(NCORES):
            b = core // 2
            vec = np.stack([_pp(g[0]), _pp(mv(b, 0)), _pp(mv(b, 1)), _pp(mv(4, 0)), _pp(mv(4, 1))], axis=1)
            maps.append({"x": xs[core], "vec": np.ascontiguousarray(vec), "w": w_in[layer], "idn": idn})
        resA = run(ncA, maps)
        maps = []
        for core in range(NCORES):
            b, hf = core // 2, core % 2
            sl = slice(hf * 512, (hf + 1) * 512)
            fT0, fT1 = resA[2 * b]["fT"], resA[2 * b + 1]["fT"]
            rg0, rg1 = resA[2 * b]["rgT"], resA[2 * b + 1]["rgT"]
            f_lat = cat([fT0[0:4096, sl], fT1[0:4096, sl]], axis=0)
            f_ctx = cat([fT0[4096:, sl], fT1[4096:, sl]], axis=0)
            full = lambda rows: cat([rg0[rows, 4096:], rg1[rows, 4096:], rg0[rows, :4096], rg1[rows, :4096]], axis=1)
            r_full = full(slice(hf * 512, hf * 512 + 512))
            g_full = full(slice(1024 + hf * 512, 1024 + hf * 512 + 512))
            m = dict(consts)
            m.update(_b_weights(hf, w_four[layer], conv_w[layer], conv_b[layer], lru_w[layer], lru_b[layer],
                                lru_lam[layer]))
            m.update(_b_acts(f_lat, f_ctx, r_full, g_full))
            maps.append(m)
        del resA
        resB = run(ncB, maps)
        ncC = build_C(32, 0 if last else 1)
        maps = []
        for core in range(NCORES):
            b, h = core // 2, core % 2
            ntok = 4096 if last else 4224
            mixT = np.empty((2048, ntok), NPBF)
            for hf in range(2):
                src = resB[2 * b + hf]["mixT"]
                lat = src[:, CTX + h * 4096:CTX + (h + 1) * 4096]
                blk = lat if last else cat([lat, src[:, h * 128:(h + 1) * 128]], axis=1)
                mixT[hf * 512:(hf + 1) * 512] = blk[0:512]
                mixT[1024 + hf * 512:1024 + (hf + 1) * 512] = blk[512:1024]
            vec = np.stack([_pp(g[2]), _pp(mv(b, 3)), _pp(mv(b, 4)), _pp(mv(4, 3)), _pp(mv(4, 4))], axis=1)
            bc = lambda v: np.broadcast_to(f32(v), (128, D))
            rows = np.stack([bc(g[1]), bc(mv(b, 2)), bc(g[3]), bc(mv(b, 5)), bc(mv(4, 2)), bc(mv(4, 5))], axis=0)
            maps.append({"mixT": mixT, "x": np.ascontiguousarray(xs[core][:ntok]), "vec": np.ascontiguousarray(vec),
                         "rows": np.ascontiguousarray(rows), "w_out": w_out[layer], "w_ffn_in": w_ffn_in[layer],
                         "w_ffn_out": w_ffn_out[layer], "idn": idn})
        del resB
        resC = run(ncC, maps)
        xs = [resC[core]["xo"] for core in range(NCORES)]
    out = np.empty((NB, SEQ, D), np.float32)
    for core in range(NCORES):
        b, h = core // 2, core % 2
        out[b, h * 4096:(h + 1) * 4096] = xs[core][:4096]
    return out


FT_LAT = SEQ // 128
FT_CTX = CTX // 128


def f_blocks(with_ctx):
    bl = [(t0, 4, 0) for t0 in range(0, FT_LAT, 4)]
    if with_ctx:
        bl.append((FT_LAT, FT_CTX, 1))
    return bl


def f_tokcol(tile0):
    return CTX + tile0 * 128 if tile0 < FT_LAT else (tile0 - FT_LAT) * 128


def f_emit_M(k, csT, w_ada, b_ada, modscr):
    with k.phase():
        cs = k.sb("m_cs", [128, 16, 2], F32)
        csb = k.sb("m_csb", [128, 16, 2], BF16)
        wts = [k.sb("m_w%d" % i, [128, 16, 512], BF16) for i in range(3)]
        bt = [k.sb("m_b%d" % i, [1, 512], F32) for i in range(2)]
        ot = [k.sb("m_o%d" % i, [1, 2, 512], F32) for i in range(2)]
        k.dma("sp", cs.t[:], csT.t[:, :, :], writes=[cs])
        k.act(csb.t[:], cs.t[:], AF.Silu, [cs], [csb])
        n = 0
        for layer in range(2):
            for nb in range(6 * D // 512):
                w = wts[n % 3]
                for k0 in range(0, 16, 4):
                    k.dma("pool", w.t[:, k0:k0 + 4, :],
                          w_ada.t[layer, k0 * 128:(k0 + 4) * 128, nb * 512:(nb + 1) * 512]
                          .rearrange("(k p) c -> p k c", p=128), writes=[w], part=(k0 > 0))
                b = bt[n % 2]
                o = ot[n % 2]
                k.dma("sp", b.t[:], b_ada.t[layer:layer + 1, nb * 512:(nb + 1) * 512], writes=[b])
                for r in range(2):
                    p = k.ps()
                    for kc in range(16):
                        k.mm(p.t[0:1, :], csb.t[:, kc, r:r + 1], w.t[:, kc, :], kc == 0, kc == 15, [csb, w], [p])
                    k.tt("dve", o.t[0:1, r, :], p.t[0:1, :], b.t[0:1, :], ALU.add, [p, b], [o], part=(r > 0))
                c0 = layer * 6 * D + nb * 512
                for r in range(2):
                    k.dma("sp", modscr.t[r:r + 1, c0:c0 + 512], o.t[0:1, r, :], reads=[o], writes=[modscr], part=True)
                n += 1


def f_load_vt(k, vt, gpp, layer, gi, modscr, j0):
    k.dma("sp", vt.t[:, 0, :], gpp.t[layer, gi, :, :], writes=[vt])
    for r in range(2):
        for jj in range(2):
            off = layer * 6 * D + (j0 + jj) * D
            k.dma("sp", vt.t[:, 1 + 2 * r + jj, :], modscr.t[r, off:off + D].rearrange("(k p) -> p k", p=128),
                  writes=[vt], part=True, slow=True)


def f_emit_A(k, layer, xrow, with_ctx, gpp, w_in, idb, modscr, X0s, fctxs, rTs, gTs):
    with k.phase():
        wt = k.sb("a_wt", [128, 16, 3072], BF16)
        wk = [T(None) for _ in range(16)]
        vt = k.sb("a_vt", [128, 5, 16], F32)
        G = k.sb("a_G", [128, 2, 16], F32)
        xt = [k.sb("a_xt%d" % i, [128, D], F32) for i in range(2)]
        junk = k.sb("a_junk", [128, D], BF16)
        xb = [k.sb("a_xb%d" % i, [128, D], BF16) for i in range(2)]
        ss = [k.sb("a_ss%d" % i, [128, 4], F32) for i in range(2)]
        tmp = [k.sb("a_tmp%d" % i, [128, 8, 128], F32) for i in range(2)]
        hT = [k.sb("a_hT%d" % i, [128, 16, 512], BF16) for i in range(2)]
        stg = [k.sb("a_stg%d" % i, [128, 512], BF16) for i in range(4)]
        for kc in range(16):
            k.dma("pool", wt.t[:, kc, :], w_in.t[layer, kc * 128:(kc + 1) * 128, :], writes=[wk[kc]])
        f_load_vt(k, vt, gpp, layer, 0, modscr, 0)
        for m in range(2):
            k.stt("dve", G.t[:, m, :], vt.t[:, 2 + 2 * m, :], 1.0, vt.t[:, 0, :], ALU.add, ALU.mult, [vt], [G],
                  part=True)
        nst = 0
        for bi, (tile0, ntile, m) in enumerate(f_blocks(with_ctx)):
            hTt = hT[bi % 2]
            ntok = ntile * 128
            col0 = f_tokcol(tile0)
            for tb in range(ntile):
                ti = tile0 + tb
                s = ti % 2
                k.dma("act", xt[s].t[:], xrow(ti), writes=[xt[s]])
                norm_to_hT(k, xt[s], s, m, G.t[:, m, :], vt.t[:, 1 + 2 * m, :], hTt, tb, idb, junk, xb[s], ss[s], tmp)
            for cc in range(16):
                p = k.ps()
                for kc in range(16):
                    k.mm(p.t[:, 0:ntok], wt.t[:, kc, 1024 + cc * 128:1024 + (cc + 1) * 128], hTt.t[:, kc, 0:ntok],
                         kc == 0, kc == 15, [wk[kc], hTt], [p])
                st = stg[nst % 4]
                nst += 1
                k.copy(k.evac_eng(), st.t[:, 0:ntok], p.t[:, 0:ntok], [p], [st])
                dst = rTs if cc < 8 else gTs
                k.dma("pool", dst.t[(cc % 8) // 4, :, cc % 4, col0:col0 + ntok], st.t[:, 0:ntok], reads=[st],
                      writes=[dst], part=True)
            for tb in range(ntile):
                ti = tile0 + tb
                for hfi in range(2):
                    p = k.ps()
                    for kc in range(16):
                        k.mm(p.t[:, :], hTt.t[:, kc, tb * 128:(tb + 1) * 128], wt.t[:, kc, hfi * 512:(hfi + 1) * 512],
                             kc == 0, kc == 15, [wk[kc], hTt], [p])
                    st = stg[nst % 4]
                    nst += 1
                    k.copy(k.evac_eng(), st.t[:, :], p.t[:, :], [p], [st])
                    if m == 0:
                        k.dma("sp", X0s.t[hfi, :, :, ti, :].rearrange("a p i -> p a i"),
                              st.t[:, :].rearrange("p (a i) -> p a i", a=4), reads=[st], writes=[X0s], part=True)
                    else:
                        k.dma("pool", fctxs.t[hfi, :, ti - FT_LAT, :], st.t[:, :], reads=[st], writes=[fctxs], part=True)


def f_emit_C(k, layer, blocks, xrow, orow, mixcol, gpp, gbc, w_out, w_in, w_o2, idb, modscr, mixs, s_out, s_in,
             s_o2):
    with k.phase():
        wb = [k.sb("c_wb%d" % i, [128, 16, 512], BF16) for i in range(3)]
        aT = k.sb("c_aT", [128, 44, 512], BF16)
        mT = k.sb("c_mT", [128, 16, 512], BF16)
        hT = mT
        yb = [k.sb("c_yb%d" % i, [128, D], F32) for i in range(4)]
        xt = [k.sb("c_xt%d" % i, [128, D], F32) for i in range(2)]
        GG = [k.sb("c_GG%d" % i, [128, D], F32) for i in range(2)]
        junk = k.sb("c_junk", [128, D], BF16)
        xb = k.sb("c_xb", [128, D], BF16)
        tmp = [k.sb("c_tmp%d" % i, [128, 8, 128], F32) for i in range(2)]
        sgt = [k.sb("c_sg%d" % i, [128, 512], F32) for i in range(2)]
        ss = [k.sb("c_ss%d" % i, [128, 4], F32) for i in range(2)]
        vt = k.sb("c_vt", [128, 5, 16], F32)
        G = k.sb("c_G", [128, 2, 16], F32)
        c_out = [T(None) for _ in range(16)]
        c_in = [T(None) for _ in range(16)]
        c_o2 = [T(None) for _ in range(44)]
        chain = [T(None) for _ in range(3)]
        nch = [0]

        def pre(out_ap, in_ap, dst):
            ch = chain[nch[0] % 3]
            nch[0] += 1
            k.dma("pool", out_ap, in_ap, writes=[dst, ch])
        for kc in range(16):
            pre(s_out.t[:, :, kc, :].rearrange("n p c -> p n c"),
                w_out.t[layer, kc * 128:(kc + 1) * 128, :].rearrange("p (n c) -> p n c", n=4), c_out[kc])
        f_load_vt(k, vt, gpp, layer, 2, modscr, 3)
        for m in range(2):
            k.stt("dve", G.t[:, m, :], vt.t[:, 2 + 2 * m, :], 1.0, vt.t[:, 0, :], ALU.add, ALU.mult, [vt], [G],
                  part=True)

        def load_GG(m):
            for j in range(2):
                off = layer * 6 * D + (2 + 3 * j) * D
                k.dma("sp", GG[j].t[:], modscr.t[m:m + 1, off:off + D].to_broadcast([128, D]), writes=[GG[j]])
                k.dma("sp", xt[1].t[:], gbc.t[layer, j, :, :], writes=[xt[1]])
                k.tt("dve", GG[j].t[:], GG[j].t[:], xt[1].t[:], ALU.mult, [GG[j], xt[1]], [GG[j]])
        load_GG(0)
        for kc in range(16):
            pre(s_in.t[:, :, kc, :].rearrange("n p c -> p n c"),
                w_in.t[layer, kc * 128:(kc + 1) * 128, :].rearrange("p (n c) -> p n c", n=22), c_in[kc])
        for kc in range(44):
            pre(s_o2.t[:, kc // 11, :, kc % 11, :].rearrange("n p c -> p n c"),
                w_o2.t[layer, kc * 128:(kc + 1) * 128, :].rearrange("p (n c) -> p n c", n=4), c_o2[kc])
        nwb = [0]

        def load_piece(src_ap, nk, deps):
            w = wb[nwb[0] % 3]
            nwb[0] += 1
            k.dma("sp", w.t[:, 0:nk, :], src_ap, reads=deps, writes=[w])
            return w

        cur_m = 0
        for bi, (tile0, ntile, m) in enumerate(blocks):
            ntok = ntile * 128
            if m != cur_m:
                load_GG(m)
                cur_m = m
            for kc in range(16):
                k.dma("sp", mT.t[:, kc, 0:ntok], mixcol(kc, tile0, ntok), writes=[mT], part=(kc > 0))
            for nb in range(4):
                w = load_piece(s_out.t[nb], 16, c_out)
                for tb in range(ntile):
                    p = k.ps()
                    for kc in range(16):
                        k.mm(p.t[:, :], mT.t[:, kc, tb * 128:(tb + 1) * 128], w.t[:, kc, :], kc == 0, kc == 15,
                             [mT, w], [p])
                    k.copy("dve", yb[tb].t[:, nb * 512:(nb + 1) * 512], p.t[:, :], [p], [yb[tb]], part=True)
            for tb in range(ntile):
                ti = tile0 + tb
                s = ti % 2
                sst = ss[s]
                k.op("act", lambda e, a=junk.t[:], b_=yb[tb].t[:], c=sst.t[:, 0:1]: e.activation(
                    out=a, in_=b_, func=AF.Square, accum_out=c), [yb[tb]], [junk, sst])
                rstd_from_ss(k, sst)
                k.dma("act", xt[s].t[:], xrow(ti), writes=[xt[s]])
                k.stt("dve", yb[tb].t[:], yb[tb].t[:], sst.t[:, 2:3], GG[0].t[:], ALU.mult, ALU.mult,
                      [yb[tb], sst, GG[0]], [yb[tb]])
                k.tt("dve", xt[s].t[:], yb[tb].t[:], xt[s].t[:], ALU.add, [yb[tb], xt[s]], [xt[s]])
                k.dma("pool", orow(ti), xt[s].t[:], reads=[xt[s]], writes=[k.xo_buf], part=True)
                norm_to_hT(k, xt[s], s, m, G.t[:, m, :], vt.t[:, 1 + 2 * m, :], hT, tb, idb, junk, xb, sst, tmp)
            for mg in range(11):
                wg = load_piece(s_in.t[mg], 16, c_in)
                wu = load_piece(s_in.t[11 + mg], 16, c_in)
                for j in range(4):
                    mi = mg * 4 + j
                    pg = k.ps()
                    for kc in range(16):
                        k.mm(pg.t[:, 0:ntok], wg.t[:, kc, j * 128:(j + 1) * 128], hT.t[:, kc, 0:ntok], kc == 0,
                             kc == 15, [wg, hT], [pg])
                    pu = k.ps()
                    for kc in range(16):
                        k.mm(pu.t[:, 0:ntok], wu.t[:, kc, j * 128:(j + 1) * 128], hT.t[:, kc, 0:ntok], kc == 0,
                             kc == 15, [wu, hT], [pu])
                    sg = sgt[mi % 2]
                    k.act(sg.t[:, 0:ntok], pg.t[:, 0:ntok], AF.Silu, [pg], [sg])
                    k.tt("dve", aT.t[:, mi, 0:ntok], sg.t[:, 0:ntok], pu.t[:, 0:ntok], ALU.mult, [sg, pu], [aT],
                         part=True)
            for nb in range(4):
                banks = [k.ps() for _ in range(ntile)]
                for kh in range(4):
                    w = load_piece(s_o2.t[nb, kh], 11, c_o2)
                    for tb in range(ntile):
                        for kk in range(11):
                            k.mm(banks[tb].t[:, :], aT.t[:, kh * 11 + kk, tb * 128:(tb + 1) * 128], w.t[:, kk, :],
                                 kh == 0 and kk == 0, kh == 3 and kk == 10, [aT, w], [banks[tb]])
                for tb in range(ntile):
                    k.copy("dve", yb[tb].t[:, nb * 512:(nb + 1) * 512], banks[tb].t[:, :], [banks[tb]], [yb[tb]],
                           part=True)
            for tb in range(ntile):
                ti = tile0 + tb
                s = ti % 2
                sst = ss[s]
                k.op("act", lambda e, a=junk.t[:], b_=yb[tb].t[:], c=sst.t[:, 0:1]: e.activation(
                    out=a, in_=b_, func=AF.Square, accum_out=c), [yb[tb]], [junk, sst])
                rstd_from_ss(k, sst)
                k.dma("act", xt[s].t[:], orow(ti), reads=[k.xo_buf], writes=[xt[s]])
                k.stt("dve", yb[tb].t[:], yb[tb].t[:], sst.t[:, 2:3], GG[1].t[:], ALU.mult, ALU.mult,
                      [yb[tb], sst, GG[1]], [yb[tb]])
                k.tt("dve", xt[s].t[:], yb[tb].t[:], xt[s].t[:], ALU.add, [yb[tb], xt[s]], [xt[s]])
                k.dma("pool", orow(ti), xt[s].t[:], reads=[xt[s]], writes=[k.xo_buf], part=True)


def build_fused(split=True):
    k = K()
    x = k.din("x", [SEQ, D], F32)
    cx = k.din("ctx", [CTX, D], F32)
    csT = k.din("csT", [128, 16, 2], F32)
    w_ada = k.din("w_ada", [2, D, 6 * D], F32)
    b_ada = k.din("b_ada", [2, 6 * D], F32)
    gpp = k.din("gpp", [2, 4, 128, 16], F32)
    gbc = k.din("gbc", [2, 2, 128, D], F32)
    w_in = k.din("w_in", [2, D, 3072], F32)
    w_out = k.din("w_out", [2, 2048, D], F32)
    w_fi = k.din("w_ffn_in", [2, D, 2 * DFF], F32)
    w_fo = k.din("w_ffn_out", [2, DFF, D], F32)
    idn = k.din("idn", [128, 128], F32)
    wf = k.din("wf", [2, 2, 128, 2, 2, 256], F32)
    lw = k.din("lw", [2, 2, 128, 16, 256], F32)
    vecs = k.din("vecs", [2, 2, 128, NV], F32)
    cs64 = k.din("cs64", [128, 128], F32)
    rs = k.din("rs", [128, 2, 256], F32)
    c256 = k.din("c256", [128, 2, 512], F32)
    out = k.dout("out", [SEQ // 2 if split else SEQ, D], F32)
    modscr = k.dscr("modscr", [2, 2 * 6 * D], F32)
    X0s = k.dscr("X0s", [2, 4, 128, 64, 128], BF16)
    fctxs = k.dscr("fctxs", [2, 128, 2, 512], BF16)
    rTs = k.dscr("rTs", [2, 128, 4, NTB], BF16)
    gTs = k.dscr("gTs", [2, 128, 4, NTB], BF16)
    mixs = k.dscr("mixs", [2048, NTB], BF16)
    hf_scr = k.dscr("hf_scr", [128, 4, NTB], F32)
    xscr = k.dscr("xscr", [SEQ + CTX, D], F32)
    s_out = k.dscr("s_out", [4, 128, 16, 512], BF16)
    s_in = k.dscr("s_in", [22, 128, 16, 512], BF16)
    s_o2 = k.dscr("s_o2", [4, 4, 128, 11, 512], BF16)
    xh = k.dscr("xh", [SEQ // 2, D], F32)
    mixh = k.dscr("mixh", [2048, SEQ // 2], BF16)
    k.xo_buf = T(None)
    k.psum_pool(8)
    idf = k.sb("idf", [128, 128], F32)
    idb = k.sb("idb", [128, 128], BF16)
    k.dma("sp", idf.t[:], idn.t[:, :], writes=[idf])
    k.copy("dve", idb.t[:], idf.t[:], [idf], [idb])
    f_emit_M(k, csT, w_ada, b_ada, modscr)
    for layer in range(2):
        last = layer == 1
        if layer == 0:
            xrow = lambda ti: (x.t[ti * 128:(ti + 1) * 128, :] if ti < FT_LAT
                               else cx.t[(ti - FT_LAT) * 128:(ti - FT_LAT + 1) * 128, :])
        else:
            xrow = lambda ti: xscr.t[ti * 128:(ti + 1) * 128, :]
        f_emit_A(k, layer, xrow, True, gpp, w_in, idb, modscr, X0s, fctxs, rTs, gTs)
        for hfi in range(2):
            with k.phase():
                emit_B(k, V(X0s.t[hfi], X0s), V(fctxs.t[hfi], fctxs), V(rTs.t[hfi], rTs), V(gTs.t[hfi], gTs),
                       V(wf.t[layer, hfi], wf), V(lw.t[layer, hfi], lw), cs64, rs, c256, V(vecs.t[layer, hfi], vecs),
                       mixs, hf_scr, hfi * 512, 1024 + hfi * 512, idb)
        mixcol = lambda kc, tile0, ntok: mixs.t[kc * 128:(kc + 1) * 128, f_tokcol(tile0):f_tokcol(tile0) + ntok]
        blocks = f_blocks(not last)
        xrow_c = xrow
        if last:
            orow = lambda ti: out.t[ti * 128:(ti + 1) * 128, :]
            if split:
                hh = k.nc.partition_id() % 2
                av = xscr.t[0:SEQ, :].rearrange("(h t) d -> h t d", h=2)
                mv = mixs.t[:, CTX:NTB].rearrange("r (h t) -> r h t", h=2)
                with k.phase():
                    for i in range(4):
                        k.dma("sp", xh.t[i * 1024:(i + 1) * 1024, :],
                              lambda i=i: av[bass.ds(hh, 1), i * 1024:(i + 1) * 1024, :].squeeze(0), writes=[xh],
                              part=True)
                        k.dma("sp", mixh.t[i * 512:(i + 1) * 512, :],
                              lambda i=i: mv[i * 512:(i + 1) * 512, bass.ds(hh, 1), :].squeeze(1), writes=[mixh],
                              part=True)
                blocks = [(t0, 4, 0) for t0 in range(0, FT_LAT // 2, 4)]
                xrow_c = lambda ti: xh.t[ti * 128:(ti + 1) * 128, :]
                mixcol = lambda kc, tile0, ntok: mixh.t[kc * 128:(kc + 1) * 128, tile0 * 128:tile0 * 128 + ntok]
        else:
            orow = lambda ti: xscr.t[ti * 128:(ti + 1) * 128, :]
        f_emit_C(k, layer, blocks, xrow_c, orow, mixcol, gpp, gbc, w_out, w_fi, w_fo, idb, modscr, mixs, s_out, s_in,
                 s_o2)
    k.outs = [out, k.xo_buf]
    return k.finish()


def kernel(x, c, ctx, c_ctx, w_ada, b_ada, norm_g, w_in, w_four, conv_w, conv_b, lru_w, lru_b, lru_lam,
           w_out, w_ffn_in, w_ffn_out):
    f32 = lambda a: np.ascontiguousarray(np.asarray(a, np.float32))
    x, ctx, c, c_ctx, norm_g = f32(x), f32(ctx), f32(c), f32(c_ctx), f32(norm_g)
    shared = {"w_ada": f32(w_ada), "b_ada": f32(b_ada), "w_in": f32(w_in), "w_out": f32(w_out),
              "w_ffn_in": f32(w_ffn_in), "w_ffn_out": f32(w_ffn_out), "idn": np.eye(128, dtype=np.float32)}
    shared.update(_b_consts())
    shared["gpp"] = np.ascontiguousarray(np.stack([np.stack([_pp(norm_g[l, n]) for n in range(4)]) for l in range(2)]))
    shared["gbc"] = np.ascontiguousarray(np.stack([np.stack([np.broadcast_to(norm_g[l, n], (128, D)) for n in (1, 3)])
                                                   for l in range(2)]))
    bw = [[_b_weights(hf, w_four[l], conv_w[l], conv_b[l], lru_w[l], lru_b[l], lru_lam[l]) for hf in range(2)]
          for l in range(2)]
    for key in ("wf", "lw", "vecs"):
        shared[key] = np.ascontiguousarray(np.stack([np.stack([bw[l][hf][key] for hf in range(2)]) for l in range(2)]))
    maps = []
    for core in range(NCORES):
        b = core // 2
        cs = np.stack([c[b], c_ctx], axis=0)
        m = dict(shared)
        m["x"] = x[b]
        m["ctx"] = ctx[b]
        m["csT"] = np.ascontiguousarray(cs.T.reshape(16, 128, 2).transpose(1, 0, 2))
        maps.append(m)
    nc = build_fused(split=True)
    res = run_bass_kernel_spmd(nc, maps, core_ids=list(range(NCORES))).results
    out = np.empty((NB, SEQ, D), np.float32)
    for core in range(NCORES):
        b, h = core // 2, core % 2
        out[b, h * (SEQ // 2):(h + 1) * (SEQ // 2)] = res[core]["out"]
    return out
```

```python
import contextlib
import numpy as np
import ml_dtypes
import concourse.bass as bass
import concourse.mybir as mybir
from concourse.bass_utils import run_bass_kernel_spmd

F32 = mybir.dt.float32
BF16 = mybir.dt.bfloat16
AF = mybir.ActivationFunctionType
ALU = mybir.AluOpType
AX = mybir.AxisListType
NPBF = ml_dtypes.bfloat16

D = 2048
SEQ = 8192
CTX = 256
NB = 4
DFF = 5632
EPS = 1e-6
NCORES = 8


class Buf:
    __slots__ = ("name", "w", "r")

    def __init__(self, name=""):
        self.name = name
        self.w = {}
        self.r = {}


class Sched:
    COMPUTE = ("pe", "act", "dve", "pool")

    def __init__(self, nc, n_dma_sems=24):
        self.nc = nc
        self.es = contextlib.ExitStack()
        self.streams = {e: [] for e in ("pe", "act", "dve", "pool", "sp")}
        self.sems = {}
        for e in self.COMPUTE:
            self.sems[e] = self.es.enter_context(nc.semaphore("sem_" + e))
        self.cnt = {e: 0 for e in self.COMPUTE}
        self.dsem = []
        for i in range(n_dma_sems):
            k = "d%d" % i
            self.sems[k] = self.es.enter_context(nc.semaphore("sem_" + k))
            self.dsem.append([k, 0])
        self.dnext = 0
        self.seen = {e: {} for e in self.streams}
        self.n_ins = 0
        self.n_wait = 0

    def op(self, eng, fn, reads=(), writes=(), dma=False, nosync_self=False, part=False):
        best = {}

        def need(d):
            for k, v in d.items():
                if nosync_self and k == eng:
                    continue
                if v > best.get(k, 0):
                    best[k] = v
        for b in reads:
            need(b.w)
        for b in writes:
            need(b.w)
            need(b.r)
        if dma:
            slot = self.dsem[self.dnext]
            self.dnext = (self.dnext + 1) % len(self.dsem)
            if slot[1] > best.get(slot[0], 0):
                best[slot[0]] = slot[1]
        waits = []
        seen = self.seen[eng]
        for k, v in best.items():
            if seen.get(k, 0) < v:
                seen[k] = v
                waits.append((k, v))
        if dma:
            slot[1] += 16
            tk, tv, inc = slot[0], slot[1], 16
        else:
            self.cnt[eng] += 1
            tk, tv, inc = eng, self.cnt[eng], 1
        self.streams[eng].append((waits, fn, tk, inc))
        self.n_ins += 1
        self.n_wait += len(waits)
        for b in reads:
            if b.r.get(tk, 0) < tv:
                b.r[tk] = tv
        for b in writes:
            if part:
                if b.w.get(tk, 0) < tv:
                    b.w[tk] = tv
            else:
                b.w = {tk: tv}
                b.r = {}

    def barrier(self):
        tgt = {e: self.cnt[e] for e in self.COMPUTE if self.cnt[e] > 0}
        for k_, tot in self.dsem:
            if tot > 0:
                tgt[k_] = tot
        for eng in self.streams:
            seen = self.seen[eng]
            waits = []
            for k_, v in tgt.items():
                if seen.get(k_, 0) < v:
                    seen[k_] = v
                    waits.append((k_, v))
            self.streams[eng].append((waits, None, None, 0))

    def finish_wait(self, eng, bufs):
        best = {}
        for b in bufs:
            for d in (b.w, b.r):
                for k, v in d.items():
                    if v > best.get(k, 0):
                        best[k] = v
        self.streams[eng].append((list(best.items()), None, None, 0))

    def emit(self):
        nc = self.nc
        sems = self.sems
        streams = self.streams
        with nc.Block() as block:
            def run(e, stream):
                for waits, fn, sk, inc in stream:
                    for k, v in waits:
                        e.wait_ge(sems[k], v)
                    if fn is not None:
                        fn(e).then_inc(sems[sk], inc)

            @block.sync
            def _(e):
                run(e, streams["sp"])

            @block.tensor
            def _(e):
                run(e, streams["pe"])

            @block.scalar
            def _(e):
                run(e, streams["act"])

            @block.vector
            def _(e):
                run(e, streams["dve"])

            @block.gpsimd
            def _(e):
                run(e, streams["pool"])


class T:
    __slots__ = ("t", "b")

    def __init__(self, t, name=""):
        self.t = t
        self.b = Buf(name)


class K:
    def __init__(self):
        self.nc = bass.Bass("TRN2", target_bir_lowering=False)
        self.S = Sched(self.nc)
        self.es = self.S.es
        self.outs = []
        self.nps = 0
        self.psl = []
        self.evq = 0
        self.phase_id = 0

    def sb(self, name, shape, dt):
        name = "%s_p%d" % (name, self.phase_id)
        return T(self.es.enter_context(self.nc.sbuf_tensor(name, list(shape), dt)), name)

    def din(self, name, shape, dt):
        return T(self.nc.dram_tensor(name, list(shape), dt, kind="ExternalInput").ap(), name)

    def dout(self, name, shape, dt):
        t = T(self.nc.dram_tensor(name, list(shape), dt, kind="ExternalOutput").ap(), name)
        self.outs.append(t)
        return t

    def dscr(self, name, shape, dt):
        return T(self.nc.dram_tensor(name, list(shape), dt, kind="Internal").ap(), name)

    @contextlib.contextmanager
    def phase(self):
        outer = self.es
        self.es = contextlib.ExitStack()
        self.phase_id += 1
        try:
            yield
        finally:
            self.S.barrier()
            self.es.close()
            self.es = outer

    def psum_pool(self, n=8):
        self.psl = [T(self.es.enter_context(self.nc.psum_tensor("ps%d" % i, [128, 512], F32)), "ps%d" % i)
                    for i in range(n)]

    def ps(self):
        p = self.psl[self.nps % len(self.psl)]
        self.nps += 1
        return p

    def op(self, eng, fn, reads=(), writes=(), **kw):
        self.S.op(eng, fn, [x.b for x in reads], [x.b for x in writes], **kw)

    def dma(self, q, out_ap, in_ap, reads=(), writes=(), part=False, slow=False):
        if callable(out_ap) or callable(in_ap):
            ev = lambda a: a() if callable(a) else a
            self.op(q, lambda e: e.dma_start(out=ev(out_ap), in_=ev(in_ap)), reads, writes, dma=True, part=part)
            return
        if slow:
            self.op(q, lambda e: e.dma_start(out=out_ap, in_=in_ap, allow_slow_non_contiguous=True), reads, writes,
                    dma=True, part=part)
        else:
            self.op(q, lambda e: e.dma_start(out=out_ap, in_=in_ap), reads, writes, dma=True, part=part)

    def mm(self, out_ap, lhsT, rhs, start, stop, reads, writes, part=None):
        self.op("pe", lambda e: e.matmul(out_ap, lhsT=lhsT, rhs=rhs, start=start, stop=stop),
                reads, writes, nosync_self=True, part=(not start) if part is None else part)

    def evac_eng(self):
        self.evq += 1
        return "act" if self.evq % 2 else "dve"

    def copy(self, eng, out_ap, in_ap, reads, writes, part=False, scale=None):
        if eng == "act":
            if scale is None:
                self.op("act", lambda e: e.activation(out=out_ap, in_=in_ap, func=AF.Copy), reads, writes, part=part)
            else:
                self.op("act", lambda e: e.activation(out=out_ap, in_=in_ap, func=AF.Copy, scale=scale),
                        reads, writes, part=part)
        else:
            if scale is None:
                self.op(eng, lambda e: e.tensor_copy(out=out_ap, in_=in_ap), reads, writes, part=part)
            else:
                self.op(eng, lambda e: e.tensor_scalar(out=out_ap, in0=in_ap, scalar1=float(scale), scalar2=None,
                                                       op0=ALU.mult), reads, writes, part=part)


    def act(self, out, in_, func, reads, writes, scale=None, bias=None, part=False):
        kw = {}
        if scale is not None:
            kw["scale"] = scale
        if bias is not None:
            kw["bias"] = bias
        self.op("act", lambda e: e.activation(out=out, in_=in_, func=func, **kw), reads, writes, part=part)

    def tt(self, eng, out, in0, in1, op, reads, writes, part=False):
        self.op(eng, lambda e: e.tensor_tensor(out=out, in0=in0, in1=in1, op=op), reads, writes, part=part)

    def ts(self, eng, out, in0, s1, s2, op0, op1, reads, writes, part=False):
        if s2 is None:
            self.op(eng, lambda e: e.tensor_scalar(out=out, in0=in0, scalar1=s1, scalar2=None, op0=op0),
                    reads, writes, part=part)
        else:
            self.op(eng, lambda e: e.tensor_scalar(out=out, in0=in0, scalar1=s1, scalar2=s2, op0=op0, op1=op1),
                    reads, writes, part=part)

    def stt(self, eng, out, in0, scalar, in1, op0, op1, reads, writes, part=False):
        self.op(eng, lambda e: e.scalar_tensor_tensor(out=out, in0=in0, scalar=scalar, in1=in1, op0=op0, op1=op1),
                reads, writes, part=part)

    def scan(self, out, d0, d1, init, reads, writes):
        self.op("dve", lambda e: e.tensor_tensor_scan(out=out, data0=d0, data1=d1, initial=init, op0=ALU.mult,
                                                      op1=ALU.add), reads, writes)

    def memset(self, eng, ap, val, writes, part=False):
        self.op(eng, lambda e: e.memset(ap, val), [], writes, part=part)

    def finish(self):
        self.S.finish_wait("sp", [t.b for t in self.outs])
        self.S.finish_wait("pool", [t.b for t in self.outs])
        self.S.emit()
        self.es.close()
        return self.nc


def run(nc, in_maps):
    res = run_bass_kernel_spmd(nc, in_maps, core_ids=list(range(len(in_maps))))
    return res.results


MCOLS = 2 * 6 * D // NCORES


def build_M():
    k = K()
    nc = k.nc
    csT = k.din("csT", [128, 16, 5], F32)
    w = k.din("w", [D, MCOLS], F32)
    b = k.din("b", [5, MCOLS], F32)
    o = k.dout("mod", [5, MCOLS], F32)
    k.psum_pool(4)
    cs = k.sb("cs", [128, 16, 5], F32)
    csb = k.sb("csb", [128, 16, 5], BF16)
    wt = k.sb("wt", [128, 16, MCOLS], BF16)
    bt = k.sb("bt", [5, MCOLS], F32)
    ot = k.sb("ot", [5, MCOLS], F32)
    k.dma("sp", cs.t[:], csT.t[:, :, :], writes=[cs])
    k.dma("sp", bt.t[:], b.t[:, :], writes=[bt])
    wk = [Buf() for _ in range(16)]
    for kc in range(16):
        k.S.op("pool", lambda e, kc=kc: e.dma_start(out=wt.t[:, kc, :], in_=w.t[kc * 128:(kc + 1) * 128, :]),
               [], [wk[kc]], dma=True)
    k.op("act", lambda e: e.activation(out=csb.t[:], in_=cs.t[:], func=AF.Silu), [cs], [csb])
    for nb in range(MCOLS // 512):
        p = k.ps()
        for kc in range(16):
            k.S.op("pe", lambda e, kc=kc, nb=nb, p=p: e.matmul(p.t[0:5, :], lhsT=csb.t[:, kc, :],
                                                                rhs=wt.t[:, kc, nb * 512:(nb + 1) * 512],
                                                                start=(kc == 0), stop=(kc == 15)),
                   [csb.b, wk[kc]], [p.b], nosync_self=True, part=(kc > 0))
        k.op("dve", lambda e, nb=nb, p=p: e.tensor_tensor(out=ot.t[0:5, nb * 512:(nb + 1) * 512], in0=p.t[0:5, :],
                                                          in1=bt.t[0:5, nb * 512:(nb + 1) * 512], op=ALU.add),
             [p, bt], [ot], part=True)
    k.dma("sp", o.t[:, :], ot.t[:], reads=[ot], writes=[o])
    return k.finish()


def rstd_from_ss(k, ss):
    k.op("dve", lambda e: e.tensor_scalar(out=ss.t[:, 1:2], in0=ss.t[:, 0:1], scalar1=1.0 / D, scalar2=EPS,
                                          op0=ALU.mult, op1=ALU.add), [ss], [ss])
    k.op("act", lambda e: e.activation(out=ss.t[:, 3:4], in_=ss.t[:, 1:2], func=AF.Sqrt), [ss], [ss])
    k.op("dve", lambda e: e.reciprocal(out=ss.t[:, 2:3], in_=ss.t[:, 3:4]), [ss], [ss])


def norm_to_hT(k, xsrc, s, m, G, shv, hTt, tb, idb, junk, xb, ss, tmp):
    k.op("act", lambda e: e.activation(out=junk.t[:], in_=xsrc.t[:], func=AF.Square, accum_out=ss.t[:, 0:1]),
         [xsrc], [junk, ss])
    rstd_from_ss(k, ss)
    k.op("act", lambda e: e.activation(out=xb.t[:], in_=xsrc.t[:], func=AF.Copy, scale=ss.t[:, 2:3]),
         [xsrc, ss], [xb])
    for half in range(2):
        p = k.ps()
        pb = p.t[:].bitcast(BF16)
        for kk in range(8):
            kc = half * 8 + kk
            k.op("pe", lambda e, kk=kk, kc=kc, pb=pb: e.transpose(out=pb[:, kk * 128:(kk + 1) * 128],
                                                                   in_=xb.t[:, kc * 128:(kc + 1) * 128],
                                                                   identity=idb.t[:]),
                 [xb, idb], [p], nosync_self=True, part=(kk > 0))
        pv = pb[:, 0:1024].rearrange("p (k t) -> p k t", k=8)
        gb = G[:, half * 8:(half + 1) * 8].unsqueeze(2).to_broadcast([128, 8, 128])
        sbv = shv[:, half * 8:(half + 1) * 8].unsqueeze(2).to_broadcast([128, 8, 128])
        tm = tmp[half]
        k.op("dve", lambda e, pv=pv, gb=gb, tm=tm: e.tensor_tensor(out=tm.t[:], in0=pv, in1=gb, op=ALU.mult),
             [p], [tm])
        k.op("dve", lambda e, sbv=sbv, tm=tm, half=half: e.tensor_tensor(
            out=hTt.t[:, half * 8:(half + 1) * 8, tb * 128:(tb + 1) * 128], in0=tm.t[:], in1=sbv, op=ALU.add),
             [tm], [hTt], part=True)


def build_A(NT_lat=32, n_ctx=1):
    k = K()
    NT = NT_lat + n_ctx
    x = k.din("x", [NT * 128, D], F32)
    vec = k.din("vec", [128, 5, 16], F32)
    w = k.din("w", [D, 3072], F32)
    idn = k.din("idn", [128, 128], F32)
    fT = k.dout("fT", [NT * 128, 1024], BF16)
    rgT = k.dout("rgT", [2048, NT * 128], BF16)
    k.psum_pool(8)
    wt = k.sb("wt", [128, 16, 3072], BF16)
    wk = [T(None) for _ in range(16)]
    vt = k.sb("vt", [128, 5, 16], F32)
    G = k.sb("G", [128, 2, 16], F32)
    idf = k.sb("idf", [128, 128], F32)
    idb = k.sb("idb", [128, 128], BF16)
    xt = [k.sb("xt%d" % i, [128, D], F32) for i in range(2)]
    junk = k.sb("junk", [128, D], BF16)
    xb = [k.sb("xb%d" % i, [128, D], BF16) for i in range(2)]
    ss = [k.sb("ss%d" % i, [128, 4], F32) for i in range(2)]
    tmp = [k.sb("tmp%d" % i, [128, 8, 128], F32) for i in range(2)]
    hT = [k.sb("hT%d" % i, [128, 16, 512], BF16) for i in range(2)]
    stg = [k.sb("stg%d" % i, [128, 512], BF16) for i in range(4)]
    k.dma("sp", vt.t[:], vec.t[:, :, :], writes=[vt])
    k.dma("sp", idf.t[:], idn.t[:, :], writes=[idf])
    for kc in range(16):
        k.dma("pool", wt.t[:, kc, :], w.t[kc * 128:(kc + 1) * 128, :], writes=[wk[kc]])
    k.op("dve", lambda e: e.tensor_copy(out=idb.t[:], in_=idf.t[:]), [idf], [idb])
    for m in range(2):
        k.op("dve", lambda e, m=m: e.scalar_tensor_tensor(out=G.t[:, m, :], in0=vt.t[:, 2 + 2 * m, :], scalar=1.0,
                                                          in1=vt.t[:, 0, :], op0=ALU.add, op1=ALU.mult),
             [vt], [G], part=True)
    blocks = []
    t0 = 0
    while t0 < NT_lat:
        n = min(4, NT_lat - t0)
        blocks.append((t0, n, 0))
        t0 += n
    for c in range(n_ctx):
        blocks.append((NT_lat + c, 1, 1))
    nst = 0
    for bi, (tile0, ntile, m) in enumerate(blocks):
        hTt = hT[bi % 2]
        ntok = ntile * 128
        tok0 = tile0 * 128
        for tb in range(ntile):
            ti = tile0 + tb
            s = ti % 2
            k.dma("sp", xt[s].t[:], x.t[ti * 128:(ti + 1) * 128, :], writes=[xt[s]])
            norm_to_hT(k, xt[s], s, m, G.t[:, m, :], vt.t[:, 1 + 2 * m, :], hTt, tb, idb, junk, xb[s], ss[s], tmp)
        for cc in range(16):
            p = k.ps()
            for kc in range(16):
                k.mm(p.t[:, 0:ntok], wt.t[:, kc, 1024 + cc * 128:1024 + (cc + 1) * 128], hTt.t[:, kc, 0:ntok],
                     kc == 0, kc == 15, [wk[kc], hTt], [p])
            st = stg[nst % 4]
            nst += 1
            k.copy(k.evac_eng(), st.t[:, 0:ntok], p.t[:, 0:ntok], [p], [st])
            k.dma("pool", rgT.t[cc * 128:(cc + 1) * 128, tok0:tok0 + ntok], st.t[:, 0:ntok], reads=[st],
                  writes=[rgT], part=True)
        for tb in range(ntile):
            for nb in range(2):
                p = k.ps()
                for kc in range(16):
                    k.mm(p.t[:, :], hTt.t[:, kc, tb * 128:(tb + 1) * 128], wt.t[:, kc, nb * 512:(nb + 1) * 512],
                         kc == 0, kc == 15, [wk[kc], hTt], [p])
                st = stg[nst % 4]
                nst += 1
                k.copy(k.evac_eng(), st.t[:, :], p.t[:, :], [p], [st])
                k.dma("pool", fT.t[tok0 + tb * 128:tok0 + (tb + 1) * 128, nb * 512:(nb + 1) * 512], st.t[:, :],
                      reads=[st], writes=[fT], part=True)
    return k.finish()


def build_C(NT_lat=32, n_ctx=1, stage=9, ctxstage=9):
    k = K()
    NT = NT_lat + n_ctx
    NTOK = NT * 128
    mixT = k.din("mixT", [2048, NTOK], BF16)
    x = k.din("x", [NTOK, D], F32)
    vec = k.din("vec", [128, 5, 16], F32)
    rows = k.din("rows", [6, 128, D], F32)
    w_out = k.din("w_out", [2048, 2048], F32)
    w_in = k.din("w_ffn_in", [2048, 2 * DFF], F32)
    w_o2 = k.din("w_ffn_out", [DFF, 2048], F32)
    idn = k.din("idn", [128, 128], F32)
    xo = k.dout("xo", [NTOK, D], F32)
    s_out = k.dscr("s_out", [2048, 2048], BF16)
    s_in = k.dscr("s_in", [2048, 2 * DFF], BF16)
    s_o2 = k.dscr("s_o2", [DFF, 2048], BF16)
    k.psum_pool(8)
    wb = [k.sb("wb%d" % i, [128, 16, 512], BF16) for i in range(3)]
    aT = k.sb("aT", [128, 44, 512], BF16)
    mT = k.sb("mT", [128, 16, 512], BF16)
    hT = mT
    yb = [k.sb("yb%d" % i, [128, D], F32) for i in range(4)]
    xt = [k.sb("xt%d" % i, [128, D], F32) for i in range(2)]
    GG = [k.sb("GG%d" % i, [128, D], F32) for i in range(2)]
    junk = k.sb("junk", [128, D], BF16)
    xb = k.sb("xb", [128, D], BF16)
    tmp = [k.sb("tmp%d" % i, [128, 8, 128], F32) for i in range(2)]
    sgt = [k.sb("sg%d" % i, [128, 512], F32) for i in range(2)]
    ssq = [k.sb("ssq%d" % i, [128, 4], F32) for i in range(4)]
    ss = [k.sb("ss%d" % i, [128, 4], F32) for i in range(2)]
    vt = k.sb("vt", [128, 5, 16], F32)
    G = k.sb("G", [128, 2, 16], F32)
    idf = k.sb("idf", [128, 128], F32)
    idb = k.sb("idb", [128, 128], BF16)
    c_out = [T(None) for _ in range(16)]
    c_in = [T(None) for _ in range(16)]
    c_o2 = [T(None) for _ in range(44)]
    for kc in range(16):
        k.dma("pool", s_out.t[kc * 128:(kc + 1) * 128, :], w_out.t[kc * 128:(kc + 1) * 128, :], writes=[c_out[kc]])
    k.dma("sp", vt.t[:], vec.t[:, :, :], writes=[vt])
    k.dma("sp", idf.t[:], idn.t[:, :], writes=[idf])
    k.op("dve", lambda e: e.tensor_copy(out=idb.t[:], in_=idf.t[:]), [idf], [idb])
    for m in range(2):
        k.op("dve", lambda e, m=m: e.scalar_tensor_tensor(out=G.t[:, m, :], in0=vt.t[:, 2 + 2 * m, :], scalar=1.0,
                                                          in1=vt.t[:, 0, :], op0=ALU.add, op1=ALU.mult),
             [vt], [G], part=True)

    def load_GG(m):
        for j in range(2):
            k.dma("sp", GG[j].t[:], rows.t[1 + 2 * j + 4 * m if False else (1 + 2 * j if m == 0 else 4 + j), :, :],
                  writes=[GG[j]])
            k.dma("sp", xt[1].t[:], rows.t[2 * j, :, :], writes=[xt[1]])
            k.op("dve", lambda e, j=j: e.tensor_tensor(out=GG[j].t[:], in0=GG[j].t[:], in1=xt[1].t[:], op=ALU.mult),
                 [GG[j], xt[1]], [GG[j]])
    load_GG(0)
    if stage >= 2:
        for kc in range(16):
            k.dma("pool", s_in.t[kc * 128:(kc + 1) * 128, :], w_in.t[kc * 128:(kc + 1) * 128, :], writes=[c_in[kc]])
        for kc in range(44):
            k.dma("pool", s_o2.t[kc * 128:(kc + 1) * 128, :], w_o2.t[kc * 128:(kc + 1) * 128, :], writes=[c_o2[kc]])
    nwb = [0]

    def load_piece(src_ap, nk, deps):
        w = wb[nwb[0] % 3]
        nwb[0] += 1
        k0 = 0
        first = True
        while k0 < nk:
            k1 = min(nk, k0 + 4)
            k.dma("sp", w.t[:, k0:k1, :], src_ap[:, k0:k1, :], reads=deps, writes=[w], part=not first)
            first = False
            k0 = k1
        return w

    blocks = []
    t0 = 0
    while t0 < NT_lat:
        n = min(4, NT_lat - t0)
        blocks.append((t0, n, 0))
        t0 += n
    for c in range(n_ctx):
        blocks.append((NT_lat + c, 1, 1))
    cur_m = 0
    stage0 = stage
    for bi, (tile0, ntile, m) in enumerate(blocks):
        ntok = ntile * 128
        tok0 = tile0 * 128
        stage = ctxstage if m == 1 else stage0
        if m != cur_m and stage0 != 8:
            load_GG(m)
            cur_m = m
        for kc in range(16):
            k.dma("sp", mT.t[:, kc, 0:ntok], mixT.t[kc * 128:(kc + 1) * 128, tok0:tok0 + ntok], writes=[mT],
                  part=(kc > 0))
        if stage < 1:
            continue
        for nb in range(4):
            w = load_piece(s_out.t[:, nb * 512:(nb + 1) * 512].rearrange("(k p) c -> p k c", p=128), 16, c_out)
            for tb in range(ntile):
                p = k.ps()
                for kc in range(16):
                    k.mm(p.t[:, :], mT.t[:, kc, tb * 128:(tb + 1) * 128], w.t[:, kc, :], kc == 0, kc == 15,
                         [mT, w], [p])
                k.op("dve", lambda e, p=p, tb=tb, nb=nb: e.tensor_copy(out=yb[tb].t[:, nb * 512:(nb + 1) * 512],
                                                                      in_=p.t[:, :]), [p], [yb[tb]], part=True)
        if stage < 2:
            continue
        for tb in range(ntile):
            ti = tile0 + tb
            s = ti % 2
            sst = ss[s]
            k.op("act", lambda e, tb=tb, sst=sst: e.activation(out=junk.t[:], in_=yb[tb].t[:], func=AF.Square,
                                                              accum_out=sst.t[:, 0:1]), [yb[tb]], [junk, sst])
            rstd_from_ss(k, sst)
            k.dma("sp", xt[s].t[:], x.t[ti * 128:(ti + 1) * 128, :], writes=[xt[s]])
            k.op("dve", lambda e, tb=tb, sst=sst: e.scalar_tensor_tensor(out=yb[tb].t[:], in0=yb[tb].t[:],
                                                                        scalar=sst.t[:, 2:3], in1=GG[0].t[:],
                                                                        op0=ALU.mult, op1=ALU.mult),
                 [yb[tb], sst, GG[0]], [yb[tb]])
            k.op("pool", lambda e, tb=tb, s=s: e.tensor_tensor(out=xt[s].t[:], in0=yb[tb].t[:], in1=xt[s].t[:],
                                                               op=ALU.add), [yb[tb], xt[s]], [xt[s]])
            k.dma("sp", xo.t[ti * 128:(ti + 1) * 128, :], xt[s].t[:], reads=[xt[s]], writes=[xo], part=True)
            norm_to_hT(k, xt[s], s, m, G.t[:, m, :], vt.t[:, 1 + 2 * m, :], hT, tb, idb, junk, xb, sst, tmp)
        if stage < 3:
            continue
        for mg in range(11):
            wg = load_piece(s_in.t[:, mg * 512:(mg + 1) * 512].rearrange("(k p) c -> p k c", p=128), 16, c_in)
            wu = load_piece(s_in.t[:, DFF + mg * 512:DFF + (mg + 1) * 512].rearrange("(k p) c -> p k c", p=128),
                            16, c_in)
            for j in range(4):
                mi = mg * 4 + j
                pg = k.ps()
                for kc in range(16):
                    k.mm(pg.t[:, 0:ntok], wg.t[:, kc, j * 128:(j + 1) * 128], hT.t[:, kc, 0:ntok], kc == 0, kc == 15,
                         [wg, hT], [pg])
                pu = k.ps()
                for kc in range(16):
                    k.mm(pu.t[:, 0:ntok], wu.t[:, kc, j * 128:(j + 1) * 128], hT.t[:, kc, 0:ntok], kc == 0, kc == 15,
                         [wu, hT], [pu])
                sg = sgt[mi % 2]
                k.op("act", lambda e, pg=pg, sg=sg, ntok=ntok: e.activation(out=sg.t[:, 0:ntok], in_=pg.t[:, 0:ntok],
                                                                func=AF.Silu), [pg], [sg])
                k.op("dve", lambda e, pu=pu, sg=sg, mi=mi, ntok=ntok: e.tensor_tensor(out=aT.t[:, mi, 0:ntok],
                                                                           in0=sg.t[:, 0:ntok], in1=pu.t[:, 0:ntok],
                                                                           op=ALU.mult), [sg, pu], [aT], part=True)
        if stage < 4:
            continue
        for nb in range(4):
            banks = [k.ps() for _ in range(ntile)]
            for kh in range(4):
                w = load_piece(s_o2.t[kh * 11 * 128:(kh + 1) * 11 * 128, nb * 512:(nb + 1) * 512]
                               .rearrange("(k p) c -> p k c", p=128), 11, c_o2)
                for tb in range(ntile):
                    for kk in range(11):
                        k.mm(banks[tb].t[:, :], aT.t[:, kh * 11 + kk, tb * 128:(tb + 1) * 128], w.t[:, kk, :],
                             kh == 0 and kk == 0, kh == 3 and kk == 10, [aT, w], [banks[tb]])
            for tb in range(ntile):
                p = banks[tb]
                k.op("dve", lambda e, p=p, tb=tb, nb=nb: e.tensor_copy(out=yb[tb].t[:, nb * 512:(nb + 1) * 512],
                                                                      in_=p.t[:, :]), [p], [yb[tb]], part=True)
        if stage < 5:
            continue
        for tb in range(ntile):
            ti = tile0 + tb
            s = ti % 2
            sst = ss[s]
            k.op("act", lambda e, tb=tb, sst=sst: e.activation(out=junk.t[:], in_=yb[tb].t[:], func=AF.Square,
                                                              accum_out=sst.t[:, 0:1]), [yb[tb]], [junk, sst])
            rstd_from_ss(k, sst)
            k.dma("sp", xt[s].t[:], xo.t[ti * 128:(ti + 1) * 128, :], reads=[xo], writes=[xt[s]])
            k.op("dve", lambda e, tb=tb, sst=sst: e.scalar_tensor_tensor(out=yb[tb].t[:], in0=yb[tb].t[:],
                                                                        scalar=sst.t[:, 2:3], in1=GG[1].t[:],
                                                                        op0=ALU.mult, op1=ALU.mult),
                 [yb[tb], sst, GG[1]], [yb[tb]])
            k.op("pool", lambda e, tb=tb, s=s: e.tensor_tensor(out=xt[s].t[:], in0=yb[tb].t[:], in1=xt[s].t[:],
                                                               op=ALU.add), [yb[tb], xt[s]], [xt[s]])
            k.dma("sp", xo.t[ti * 128:(ti + 1) * 128, :], xt[s].t[:], reads=[xt[s]], writes=[xo], part=True)
    return k.finish()


NTB = CTX + SEQ
NV = 48


class V:
    __slots__ = ("t", "b")

    def __init__(self, ap, parent):
        self.t = ap
        self.b = parent.b


def build_B():
    k = K()
    X0 = k.din("X0", [4, 128, 64, 128], BF16)
    fctx = k.din("fctx", [128, 2, 512], BF16)
    rT = k.din("rT", [128, 4, NTB], BF16)
    gT = k.din("gT", [128, 4, NTB], BF16)
    wf = k.din("wf", [128, 2, 2, 256], F32)
    lw = k.din("lw", [128, 16, 256], F32)
    cs64 = k.din("cs64", [128, 128], F32)
    rs = k.din("rs", [128, 2, 256], F32)
    c256 = k.din("c256", [128, 2, 512], F32)
    vecs = k.din("vecs", [128, NV], F32)
    mixT = k.dout("mixT", [1024, NTB], BF16)
    hf_scr = k.dscr("hf_scr", [128, 4, NTB], F32)
    k.psum_pool(8)
    idn = k.din("idn", [128, 128], F32)
    idf = k.sb("idf", [128, 128], F32)
    idb = k.sb("idb", [128, 128], BF16)
    k.dma("sp", idf.t[:], idn.t[:, :], writes=[idf])
    k.copy("dve", idb.t[:], idf.t[:], [idf], [idb])
    emit_B(k, X0, fctx, rT, gT, wf, lw, cs64, rs, c256, vecs, mixT, hf_scr, 0, 512, idb)
    return k.finish()


def emit_B(k, X0, fctx, rT, gT, wf, lw, cs64, rs, c256, vecs, mixT, hf_scr, ROWF, ROWL, idb):
    vt = k.sb("vt", [128, NV], F32)
    ost = [k.sb("ost%d" % i, [128, 512], BF16) for i in range(4)]
    k.dma("sp", vt.t[:], vecs.t[:, :], writes=[vt])
    nost = [0]

    def next_ost():
        o = ost[nost[0] % 4]
        nost[0] += 1
        return o

    with k.phase():
        wfb = k.sb("wfb", [128, 2, 2, 256], BF16)
        cs64b = k.sb("cs64b", [128, 128], BF16)
        rsb = k.sb("rsb", [128, 2, 256], BF16)
        c256b = k.sb("c256b", [128, 2, 512], BF16)
        MP = k.sb("MP", [128, 2, 2, 256], BF16)
        MQ = k.sb("MQ", [128, 2, 2, 256], BF16)
        X0c = [k.sb("X0c%d" % i, [128, 64, 128], BF16) for i in range(2)]
        X1c = k.sb("X1c", [128, 128, 128], BF16)
        PQT = k.sb("PQT", [128, 2, 2, SEQ], BF16)
        PQc = k.sb("PQc", [128, 4, 512], BF16)
        fcb = k.sb("fcb", [128, 2, 512], BF16)
        for dst, src in ((wfb, wf), (cs64b, cs64), (rsb, rs), (c256b, c256)):
            k.dma("pool", dst.t[:], src.t[:], writes=[dst])
        k.dma("sp", fcb.t[:], fctx.t[:, :, :], writes=[fcb])
        for g in range(2):
            for ic in range(2):
                p = k.ps()
                for kc in range(2):
                    k.mm(p.t[:, 0:256], c256b.t[:, kc, ic * 128:(ic + 1) * 128], wfb.t[:, g, kc, :], kc == 0, kc == 1,
                         [c256b, wfb], [p], part=(kc > 0))
                for kc in range(2):
                    k.mm(p.t[:, 256:512], c256b.t[:, kc, 256 + ic * 128:256 + (ic + 1) * 128], wfb.t[:, g, kc, :],
                         kc == 0, kc == 1, [c256b, wfb], [p], part=True)
                k.copy("dve", MP.t[:, g, ic, :], p.t[:, 0:256], [p], [MP], part=True)
                k.copy("dve", MQ.t[:, g, ic, :], p.t[:, 256:512], [p], [MQ], part=True, scale=-1.0)
        for ic in range(4):
            p = k.ps()
            for lc in range(2):
                k.mm(p.t[:, :], fcb.t[:, lc, ic * 128:(ic + 1) * 128], c256b.t[:, lc, :], lc == 0, lc == 1,
                     [fcb, c256b], [p])
            k.copy(k.evac_eng(), PQc.t[:, ic, :], p.t[:, :], [p], [PQc], part=True, scale=1.0 / 256.0)
        for g in range(2):
            for jc in range(2):
                p = k.ps()
                n = 0
                for icl in range(2):
                    for pq in range(2):
                        M = MP if pq == 0 else MQ
                        k.mm(p.t[:, 0:256], M.t[:, g, icl, jc * 128:(jc + 1) * 128],
                             PQc.t[:, g * 2 + icl, pq * 256:(pq + 1) * 256], n == 0, n == 3, [M, PQc], [p])
                        n += 1
                o = next_ost()
                k.copy(k.evac_eng(), o.t[:, 0:256], p.t[:, 0:256], [p], [o])
                k.dma("pool", mixT.t[ROWF + g * 256 + jc * 128:ROWF + g * 256 + (jc + 1) * 128, 0:256], o.t[:, 0:256], reads=[o],
                      writes=[mixT], part=True)
        NORM = float((128.0 * 64.0 * 256.0) ** -0.5)
        for ic in range(4):
            g, icl = ic // 2, ic % 2
            xc = X0c[ic % 2]
            k.dma("sp", xc.t[:], X0.t[ic, :, :, :], writes=[xc])
            for i4 in range(32):
                p = k.ps()
                for ii in range(4):
                    i = i4 * 4 + ii
                    k.mm(p.t[0:64, ii * 128:(ii + 1) * 128], xc.t[0:64, :, i], cs64b.t[0:64, :], True, True,
                         [xc, cs64b], [p], part=(ii > 0))
                    k.mm(p.t[64:128, ii * 128:(ii + 1) * 128], xc.t[64:128, :, i], cs64b.t[64:128, :], True, True,
                         [xc, cs64b], [p], part=True)
                k.copy(k.evac_eng(), X1c.t[:, i4 * 4:(i4 + 1) * 4, :],
                       p.t[:, :].rearrange("p (a b) -> p a b", a=4), [p], [X1c], part=(i4 > 0))
            pqv = PQT.t[:, icl, :, :].rearrange("p a (r c) -> p a r c", c=64)
            for kc2 in range(32):
                p = k.ps()
                for q in range(2):
                    kc = kc2 * 2 + q
                    k.mm(p.t[:, q * 256:(q + 1) * 256], X1c.t[:, :, kc], rsb.t[:, 0, :], True, False, [X1c, rsb], [p],
                         part=(q > 0))
                    k.mm(p.t[:, q * 256:(q + 1) * 256], X1c.t[:, :, 64 + kc], rsb.t[:, 1, :], False, True, [X1c, rsb], [p],
                         part=True)
                for q in range(2):
                    kc = kc2 * 2 + q
                    k.copy(k.evac_eng(), pqv[:, :, :, kc], p.t[:, q * 256:(q + 1) * 256].rearrange("p (a r) -> p a r", a=2),
                           [p], [PQT], part=True, scale=NORM)
            if icl == 1:
                for tb in range(16):
                    for jc in range(2):
                        p = k.ps()
                        n = 0
                        for il in range(2):
                            for pq in range(2):
                                M = MP if pq == 0 else MQ
                                k.mm(p.t[:, :], M.t[:, g, il, jc * 128:(jc + 1) * 128],
                                     PQT.t[:, il, pq, tb * 512:(tb + 1) * 512], n == 0, n == 3, [M, PQT], [p])
                                n += 1
                        o = next_ost()
                        k.copy(k.evac_eng(), o.t[:, :], p.t[:, :], [p], [o])
                        k.dma("pool", mixT.t[ROWF + g * 256 + jc * 128:ROWF + g * 256 + (jc + 1) * 128,
                                           CTX + tb * 512:CTX + (tb + 1) * 512], o.t[:, :], reads=[o], writes=[mixT],
                              part=True)
    with k.phase():
        lwb = k.sb("lwb", [128, 16, 256], BF16)
        csv = k.sb("csv", [128, 16], F32)
        k.dma("pool", lwb.t[:], lw.t[:], writes=[lwb])
        TC = 512
        mk = lambda nm, shape, dt: [[k.sb("%s%d_%d" % (nm, par, i), shape, dt) for i in range(4)] for par in range(2)]
        R = mk("R", [128, TC + 4], BF16)
        u = mk("u", [128, TC], F32)
        ub = mk("ub", [128, TC], BF16)
        rg = mk("rg", [128, TC], F32)
        ig = mk("ig", [128, TC], F32)
        av = mk("av", [128, TC], F32)
        a2 = mk("a2", [128, TC], F32)
        hfl = mk("hfl", [128, TC], F32)
        gch = mk("gch", [128, TC], BF16)
        gl = mk("gl", [128, TC], F32)
        state = k.sb("state", [128, 8], F32)
        ONE = vt.t[:, 44:45]
        k.act(csv.t[:, 0:8], vt.t[:, 36:44], AF.Exp, [vt], [csv], scale=-1.0)
        k.op("dve", lambda e: e.tensor_scalar(out=csv.t[:, 0:8], in0=csv.t[:, 0:8], scalar1=1.0, scalar2=None,
                                              op0=ALU.add), [csv], [csv])
        k.act(csv.t[:, 0:8], csv.t[:, 0:8], AF.Ln, [csv], [csv])
        k.op("dve", lambda e: e.tensor_scalar(out=csv.t[:, 8:16], in0=csv.t[:, 0:8], scalar1=-16.0, scalar2=None,
                                              op0=ALU.mult), [csv], [csv])
        k.op("dve", lambda e: e.tensor_scalar(out=csv.t[:, 0:8], in0=csv.t[:, 0:8], scalar1=-8.0, scalar2=None,
                                              op0=ALU.mult), [csv], [csv])
        k.memset("dve", state.t[:], 0.0, [state])
        chunks = [(0, CTX, True, True)] + [(CTX + j * TC, TC, j == 0, j == SEQ // TC - 1) for j in range(SEQ // TC)]

        def conv_chunk(par, chunk, d):
            t0, Tn, first, last = chunk
            for cc in range(4):
                Rt = R[par][cc]
                lo = t0 if first else t0 - 2
                hi = t0 + Tn if last else t0 + Tn + 1
                k.memset("dve", Rt.t[:, 0:2], 0.0, [Rt])
                k.memset("dve", Rt.t[:, Tn + 2:Tn + 4], 0.0, [Rt], part=True)
                k.dma("sp", Rt.t[:, lo - (t0 - 2):hi - (t0 - 2)], rT.t[:, cc, lo:hi], writes=[Rt], part=True)
                if d == 1:
                    k.dma("sp", hfl[par][cc].t[:, 0:Tn], hf_scr.t[:, cc, t0:t0 + Tn], reads=[hf_scr],
                          writes=[hfl[par][cc]])
                    k.dma("sp", gch[par][cc].t[:, 0:Tn], gT.t[:, cc, t0:t0 + Tn], writes=[gch[par][cc]])
            for cc in range(4):
                Rt, uu = R[par][cc], u[par][cc]
                k.ts("dve", uu.t[:, 0:Tn], Rt.t[:, 0:Tn], vt.t[:, cc:cc + 1], vt.t[:, 16 + cc:17 + cc], ALU.mult,
                     ALU.add, [Rt, vt], [uu])
                for tap in range(1, 4):
                    k.stt("dve", uu.t[:, 0:Tn], Rt.t[:, tap:tap + Tn], vt.t[:, tap * 4 + cc:tap * 4 + cc + 1],
                          uu.t[:, 0:Tn], ALU.mult, ALU.add, [Rt, vt, uu], [uu])
                k.copy("act", ub[par][cc].t[:, 0:Tn], uu.t[:, 0:Tn], [uu], [ub[par][cc]])

        def act_chain(par, chunk, d):
            t0, Tn, first, last = chunk
            for cc in range(4):
                head, jc = cc // 2, cc % 2
                for gate, dst in ((0, rg), (1, ig)):
                    p = k.ps()
                    for kc in range(2):
                        wi = ((d * 2 + gate) * 2 + head) * 2 + kc
                        k.mm(p.t[:, 0:Tn], lwb.t[:, wi, jc * 128:(jc + 1) * 128], ub[par][head * 2 + kc].t[:, 0:Tn],
                             kc == 0, kc == 1, [lwb, ub[par][head * 2 + kc]], [p])
                    bcol = 20 + (d * 2 + gate) * 4 + cc
                    k.act(dst[par][cc].t[:, 0:Tn], p.t[:, 0:Tn], AF.Sigmoid, [p, vt], [dst[par][cc]],
                          bias=vt.t[:, bcol:bcol + 1])
            for cc in range(4):
                k.act(av[par][cc].t[:, 0:Tn], rg[par][cc].t[:, 0:Tn], AF.Exp, [rg[par][cc], csv], [av[par][cc]],
                      scale=csv.t[:, d * 4 + cc:d * 4 + cc + 1])
                k.act(a2[par][cc].t[:, 0:Tn], rg[par][cc].t[:, 0:Tn], AF.Exp, [rg[par][cc], csv], [a2[par][cc]],
                      scale=csv.t[:, 8 + d * 4 + cc:8 + d * 4 + cc + 1])
            for cc in range(4):
                k.act(a2[par][cc].t[:, 0:Tn], a2[par][cc].t[:, 0:Tn], AF.Sqrt, [a2[par][cc], vt], [a2[par][cc]],
                      scale=-1.0, bias=ONE)
            if d == 1:
                for cc in range(4):
                    k.act(gl[par][cc].t[:, 0:Tn], gch[par][cc].t[:, 0:Tn], AF.Gelu_apprx_tanh, [gch[par][cc]],
                          [gl[par][cc]])

        def dve_part(par, chunk, d):
            t0, Tn, first, last = chunk
            for cc in range(4):
                igt, rgt, avt = ig[par][cc], rg[par][cc], av[par][cc]
                k.tt("dve", igt.t[:, 0:Tn], igt.t[:, 0:Tn], u[par][cc].t[:, 0:Tn], ALU.mult, [igt, u[par][cc]], [igt])
                k.tt("dve", igt.t[:, 0:Tn], igt.t[:, 0:Tn], a2[par][cc].t[:, 0:Tn], ALU.mult, [igt, a2[par][cc]], [igt])
                sc = state.t[:, d * 4 + cc:d * 4 + cc + 1]
                if d == 0:
                    k.scan(rgt.t[:, 0:Tn], avt.t[:, 0:Tn], igt.t[:, 0:Tn], sc, [avt, igt, state], [rgt])
                    k.copy("dve", sc, rgt.t[:, Tn - 1:Tn], [rgt], [state])
                    k.dma("pool", hf_scr.t[:, cc, t0:t0 + Tn], rgt.t[:, 0:Tn], reads=[rgt], writes=[hf_scr], part=True)
                else:
                    k.scan(rgt.t[:, 0:Tn][:, ::-1], avt.t[:, 0:Tn][:, ::-1], igt.t[:, 0:Tn][:, ::-1], sc,
                           [avt, igt, state], [rgt])
                    k.copy("dve", sc, rgt.t[:, 0:1], [rgt], [state])
                    ht = hfl[par][cc]
                    k.tt("dve", ht.t[:, 0:Tn], ht.t[:, 0:Tn], rgt.t[:, 0:Tn], ALU.add, [ht, rgt], [ht])
                    o = next_ost()
                    k.tt("dve", o.t[:, 0:Tn], ht.t[:, 0:Tn], gl[par][cc].t[:, 0:Tn], ALU.mult, [ht, gl[par][cc]], [o])
                    k.dma("pool", mixT.t[ROWL + cc * 128:ROWL + (cc + 1) * 128, t0:t0 + Tn], o.t[:, 0:Tn], reads=[o],
                          writes=[mixT], part=True)

        for d, order in ((0, chunks), (1, [chunks[0]] + chunks[:0:-1])):
            conv_chunk(0, order[0], d)
            for n, chunk in enumerate(order):
                par = n % 2
                act_chain(par, chunk, d)
                if n + 1 < len(order):
                    conv_chunk(1 - par, order[n + 1], d)
                dve_part(par, chunk, d)
def _pp(v):
    return np.ascontiguousarray(np.asarray(v, np.float32).reshape(16, 128).T)


def _b_consts():
    c = np.arange(64)
    ang = 2.0 * np.pi * np.outer(c, c) / 64.0
    cs = np.concatenate([np.cos(ang), np.sin(ang)], axis=1)
    cs64 = np.concatenate([cs, cs], axis=0)
    p = np.arange(128)
    r = 2 * (p % 64) + p // 64
    ang = 2.0 * np.pi * np.outer(r, np.arange(128)) / 128.0
    rs1 = np.concatenate([np.cos(ang), np.sin(ang)], axis=1)
    rs2 = np.concatenate([-np.sin(ang), np.cos(ang)], axis=1)
    rs = np.stack([rs1, rs2], axis=1)
    n = np.arange(256)
    ang = 2.0 * np.pi * np.outer(n, n) / 256.0
    cc = np.concatenate([np.cos(ang), np.sin(ang)], axis=1)
    c256 = cc.reshape(2, 128, 512).transpose(1, 0, 2)
    f = lambda a: np.ascontiguousarray(a, dtype=np.float32)
    return {"cs64": f(cs64), "rs": f(rs), "c256": f(c256), "idn": np.eye(128, dtype=np.float32)}


def _b_weights(hf, w_four_l, conv_w_l, conv_b_l, lru_w_l, lru_b_l, lru_lam_l):
    wf = np.asarray(w_four_l, np.float32)[2 * hf:2 * hf + 2]
    wf = wf.reshape(2, 2, 128, 256).transpose(2, 0, 1, 3)
    lw = np.asarray(lru_w_l, np.float32)[:, :, 2 * hf:2 * hf + 2]
    lw = lw.reshape(2, 2, 2, 2, 128, 256).transpose(4, 0, 1, 2, 3, 5).reshape(128, 16, 256)
    sl = slice(hf * 512, (hf + 1) * 512)
    vec = np.zeros((128, NV), np.float32)
    col = lambda v: np.asarray(v, np.float32)[sl].reshape(4, 128).T
    for tap in range(4):
        vec[:, tap * 4:tap * 4 + 4] = col(conv_w_l[tap])
    vec[:, 16:20] = col(conv_b_l)
    for d in range(2):
        for g in range(2):
            vec[:, 20 + (d * 2 + g) * 4:24 + (d * 2 + g) * 4] = col(lru_b_l[d, g])
        vec[:, 36 + d * 4:40 + d * 4] = col(lru_lam_l[d])
    vec[:, 44] = 1.0
    return {"wf": np.ascontiguousarray(wf), "lw": np.ascontiguousarray(lw), "vecs": vec}


def _b_acts(f_lat, f_ctx, r_full, g_full):
    X0 = f_lat.reshape(64, 2, 64, 4, 128).transpose(3, 1, 2, 0, 4).reshape(4, 128, 64, 128)
    fc = f_ctx.reshape(2, 128, 512).transpose(1, 0, 2)
    rT = r_full.reshape(4, 128, NTB).transpose(1, 0, 2)
    gT = g_full.reshape(4, 128, NTB).transpose(1, 0, 2)
    c = np.ascontiguousarray
    return {"X0": c(X0), "fctx": c(fc), "rT": c(rT), "gT": c(gT)}


def _mod_rows(c, c_ctx, w_ada, b_ada):
    cs = np.concatenate([np.asarray(c, np.float32), np.asarray(c_ctx, np.float32)[None]], axis=0)
    csT = np.ascontiguousarray(cs.T.reshape(16, 128, 5).transpose(1, 0, 2))
    wcat = np.concatenate([np.asarray(w_ada[l], np.float32) for l in range(2)], axis=1)
    bcat = np.concatenate([np.asarray(b_ada[l], np.float32) for l in range(2)], axis=0)
    nc = build_M()
    maps = []
    for j in range(NCORES):
        sl = slice(j * MCOLS, (j + 1) * MCOLS)
        maps.append({"csT": csT, "w": np.ascontiguousarray(wcat[:, sl]),
                     "b": np.ascontiguousarray(np.broadcast_to(bcat[sl], (5, MCOLS)))})
    res = run(nc, maps)
    return np.concatenate([r["mod"] for r in res], axis=1)


def kernel_unfused(x, c, ctx, c_ctx, w_ada, b_ada, norm_g, w_in, w_four, conv_w, conv_b, lru_w, lru_b, lru_lam,
           w_out, w_ffn_in, w_ffn_out):
    f32 = lambda a: np.asarray(a, np.float32)
    x, ctx, norm_g = f32(x), f32(ctx), f32(norm_g)
    w_in, w_out, w_ffn_in, w_ffn_out = f32(w_in), f32(w_out), f32(w_ffn_in), f32(w_ffn_out)
    mod = _mod_rows(c, c_ctx, w_ada, b_ada)
    idn = np.eye(128, dtype=np.float32)
    consts = _b_consts()
    cat = np.concatenate
    ncA = build_A(32, 1)
    ncB = build_B()
    xs = []
    for core in range(NCORES):
        b, h = core // 2, core % 2
        xs.append(np.ascontiguousarray(cat([x[b, h * 4096:(h + 1) * 4096], ctx[b, h * 128:(h + 1) * 128]], axis=0)))
    for layer in range(2):
        last = layer == 1
        mv = lambda row, j: mod[row, layer * 6 * D + j * D:layer * 6 * D + (j + 1) * D]
        g = norm_g[layer]
        maps = []
        for core in range(NCORES):
            b = core // 2
            vec = np.stack([_pp(g[0]), _pp(mv(b, 0)), _pp(mv(b, 1)), _pp(mv(4, 0)), _pp(mv(4, 1))], axis=1)
            maps.append({"x": xs[core], "vec": np.ascontiguousarray(vec), "w": w_in[layer], "idn": idn})
        resA = run(ncA, maps)
        maps = []
        for core in range(NCORES):
            b, hf = core // 2, core % 2
            sl = slice(hf * 512, (hf + 1) * 512)
            fT0, fT1 = resA[2 * b]["fT"], resA[2 * b + 1]["fT"]
            rg0, rg1 = resA[2 * b]["rgT"], resA[2 * b + 1]["rgT"]
            f_lat = cat([fT0[0:4096, sl], fT1[0:4096, sl]], axis=0)
            f_ctx = cat([fT0[4096:, sl], fT1[4096:, sl]], axis=0)
            full = lambda rows: cat([rg0[rows, 4096:], rg1[rows, 4096:], rg0[rows, :4096], rg1[rows, :4096]], axis=1)
            r_full = full(slice(hf * 512, hf * 512 + 512))
            g_full = full(slice(1024 + hf * 512, 1024 + hf * 512 + 512))
            m = dict(consts)
            m.update(_b_weights(hf, w_four[layer], conv_w[layer], conv_b[layer], lru_w[layer], lru_b[layer],
                                lru_lam[layer]))
            m.update(_b_acts(f_lat, f_ctx, r_full, g_full))
            maps.append(m)
        del resA
        resB = run(ncB, maps)
        ncC = build_C(32, 0 if last else 1)
        maps = []
        for core in range(NCORES):
            b, h = core // 2, core % 2
            ntok = 4096 if last else 4224
            mixT = np.empty((2048, ntok), NPBF)
            for hf in range(2):
                src = resB[2 * b + hf]["mixT"]
                lat = src[:, CTX + h * 4096:CTX + (h + 1) * 4096]
                blk = lat if last else cat([lat, src[:, h * 128:(h + 1) * 128]], axis=1)
                mixT[hf * 512:(hf + 1) * 512] = blk[0:512]
                mixT[1024 + hf * 512:1024 + (hf + 1) * 512] = blk[512:1024]
            vec = np.stack([_pp(g[2]), _pp(mv(b, 3)), _pp(mv(b, 4)), _pp(mv(4, 3)), _pp(mv(4, 4))], axis=1)
            bc = lambda v: np.broadcast_to(f32(v), (128, D))
            rows = np.stack([bc(g[1]), bc(mv(b, 2)), bc(g[3]), bc(mv(b, 5)), bc(mv(4, 2)), bc(mv(4, 5))], axis=0)
            maps.append({"mixT": mixT, "x": np.ascontiguousarray(xs[core][:ntok]), "vec": np.ascontiguousarray(vec),
                         "rows": np.ascontiguousarray(rows), "w_out": w_out[layer], "w_ffn_in": w_ffn_in[layer],
                         "w_ffn_out": w_ffn_out[layer], "idn": idn})
        del resB
        resC = run(ncC, maps)
        xs = [resC[core]["xo"] for core in range(NCORES)]
    out = np.empty((NB, SEQ, D), np.float32)
    for core in range(NCORES):
        b, h = core // 2, core % 2
        out[b, h * 4096:(h + 1) * 4096] = xs[core][:4096]
    return out


FT_LAT = SEQ // 128
FT_CTX = CTX // 128


def f_blocks(with_ctx):
    bl = [(t0, 4, 0) for t0 in range(0, FT_LAT, 4)]
    if with_ctx:
        bl.append((FT_LAT, FT_CTX, 1))
    return bl


def f_tokcol(tile0):
    return CTX + tile0 * 128 if tile0 < FT_LAT else (tile0 - FT_LAT) * 128


def f_emit_M(k, csT, w_ada, b_ada, modscr):
    with k.phase():
        cs = k.sb("m_cs", [128, 16, 2], F32)
        csb = k.sb("m_csb", [128, 16, 2], BF16)
        wts = [k.sb("m_w%d" % i, [128, 16, 512], BF16) for i in range(3)]
        bt = [k.sb("m_b%d" % i, [1, 512], F32) for i in range(2)]
        ot = [k.sb("m_o%d" % i, [1, 2, 512], F32) for i in range(2)]
        k.dma("sp", cs.t[:], csT.t[:, :, :], writes=[cs])
        k.act(csb.t[:], cs.t[:], AF.Silu, [cs], [csb])
        n = 0
        for layer in range(2):
            for nb in range(6 * D // 512):
                w = wts[n % 3]
                for k0 in range(0, 16, 4):
                    k.dma("pool", w.t[:, k0:k0 + 4, :],
                          w_ada.t[layer, k0 * 128:(k0 + 4) * 128, nb * 512:(nb + 1) * 512]
                          .rearrange("(k p) c -> p k c", p=128), writes=[w], part=(k0 > 0))
                b = bt[n % 2]
                o = ot[n % 2]
                k.dma("sp", b.t[:], b_ada.t[layer:layer + 1, nb * 512:(nb + 1) * 512], writes=[b])
                for r in range(2):
                    p = k.ps()
                    for kc in range(16):
                        k.mm(p.t[0:1, :], csb.t[:, kc, r:r + 1], w.t[:, kc, :], kc == 0, kc == 15, [csb, w], [p])
                    k.tt("dve", o.t[0:1, r, :], p.t[0:1, :], b.t[0:1, :], ALU.add, [p, b], [o], part=(r > 0))
                c0 = layer * 6 * D + nb * 512
                for r in range(2):
                    k.dma("sp", modscr.t[r:r + 1, c0:c0 + 512], o.t[0:1, r, :], reads=[o], writes=[modscr], part=True)
                n += 1


def f_load_vt(k, vt, gpp, layer, gi, modscr, j0):
    k.dma("sp", vt.t[:, 0, :], gpp.t[layer, gi, :, :], writes=[vt])
    for r in range(2):
        for jj in range(2):
            off = layer * 6 * D + (j0 + jj) * D
            k.dma("sp", vt.t[:, 1 + 2 * r + jj, :], modscr.t[r, off:off + D].rearrange("(k p) -> p k", p=128),
                  writes=[vt], part=True, slow=True)


def f_emit_A(k, layer, xrow, with_ctx, gpp, w_in, idb, modscr, X0s, fctxs, rTs, gTs):
    with k.phase():
        wt = k.sb("a_wt", [128, 16, 3072], BF16)
        wk = [T(None) for _ in range(16)]
        vt = k.sb("a_vt", [128, 5, 16], F32)
        G = k.sb("a_G", [128, 2, 16], F32)
        xt = [k.sb("a_xt%d" % i, [128, D], F32) for i in range(2)]
        junk = k.sb("a_junk", [128, D], BF16)
        xb = [k.sb("a_xb%d" % i, [128, D], BF16) for i in range(2)]
        ss = [k.sb("a_ss%d" % i, [128, 4], F32) for i in range(2)]
        tmp = [k.sb("a_tmp%d" % i, [128, 8, 128], F32) for i in range(2)]
        hT = [k.sb("a_hT%d" % i, [128, 16, 512], BF16) for i in range(2)]
        stg = [k.sb("a_stg%d" % i, [128, 512], BF16) for i in range(4)]
        for kc in range(16):
            k.dma("pool", wt.t[:, kc, :], w_in.t[layer, kc * 128:(kc + 1) * 128, :], writes=[wk[kc]])
        f_load_vt(k, vt, gpp, layer, 0, modscr, 0)
        for m in range(2):
            k.stt("dve", G.t[:, m, :], vt.t[:, 2 + 2 * m, :], 1.0, vt.t[:, 0, :], ALU.add, ALU.mult, [vt], [G],
                  part=True)
        nst = 0
        for bi, (tile0, ntile, m) in enumerate(f_blocks(with_ctx)):
            hTt = hT[bi % 2]
            ntok = ntile * 128
            col0 = f_tokcol(tile0)
            for tb in range(ntile):
                ti = tile0 + tb
                s = ti % 2
                k.dma("sp", xt[s].t[:], xrow(ti), writes=[xt[s]])
                norm_to_hT(k, xt[s], s, m, G.t[:, m, :], vt.t[:, 1 + 2 * m, :], hTt, tb, idb, junk, xb[s], ss[s], tmp)
            for cc in range(16):
                p = k.ps()
                for kc in range(16):
                    k.mm(p.t[:, 0:ntok], wt.t[:, kc, 1024 + cc * 128:1024 + (cc + 1) * 128], hTt.t[:, kc, 0:ntok],
                         kc == 0, kc == 15, [wk[kc], hTt], [p])
                st = stg[nst % 4]
                nst += 1
                k.copy(k.evac_eng(), st.t[:, 0:ntok], p.t[:, 0:ntok], [p], [st])
                dst = rTs if cc < 8 else gTs
                k.dma("pool", dst.t[(cc % 8) // 4, :, cc % 4, col0:col0 + ntok], st.t[:, 0:ntok], reads=[st],
                      writes=[dst], part=True)
            for tb in range(ntile):
                ti = tile0 + tb
                for hfi in range(2):
                    p = k.ps()
                    for kc in range(16):
                        k.mm(p.t[:, :], hTt.t[:, kc, tb * 128:(tb + 1) * 128], wt.t[:, kc, hfi * 512:(hfi + 1) * 512],
                             kc == 0, kc == 15, [wk[kc], hTt], [p])
                    st = stg[nst % 4]
                    nst += 1
                    k.copy(k.evac_eng(), st.t[:, :], p.t[:, :], [p], [st])
                    if m == 0:
                        k.dma("sp", X0s.t[hfi, :, :, ti, :].rearrange("a p i -> p a i"),
                              st.t[:, :].rearrange("p (a i) -> p a i", a=4), reads=[st], writes=[X0s], part=True)
                    else:
                        k.dma("pool", fctxs.t[hfi, :, ti - FT_LAT, :], st.t[:, :], reads=[st], writes=[fctxs], part=True)


def f_emit_C(k, layer, blocks, xrow, orow, mixcol, gpp, gbc, w_out, w_in, w_o2, idb, modscr, mixs, s_out, s_in,
             s_o2):
    with k.phase():
        wb = [k.sb("c_wb%d" % i, [128, 16, 512], BF16) for i in range(4)]
        aT = k.sb("c_aT", [128, 44, 512], BF16)
        mT = k.sb("c_mT", [128, 16, 512], BF16)
        hT = mT
        yb = [k.sb("c_yb%d" % i, [128, D], F32) for i in range(4)]
        xt = [k.sb("c_xt%d" % i, [128, D], F32) for i in range(2)]
        GG = [k.sb("c_GG%d" % i, [128, D], F32) for i in range(2)]
        xb = k.sb("c_xb", [128, D], BF16)
        junk = xb
        tmp = [k.sb("c_tmp%d" % i, [128, 8, 128], F32) for i in range(2)]
        sgt = [k.sb("c_sg%d" % i, [128, 512], F32) for i in range(2)]
        ss = [k.sb("c_ss%d" % i, [128, 4], F32) for i in range(2)]
        vt = k.sb("c_vt", [128, 5, 16], F32)
        G = k.sb("c_G", [128, 2, 16], F32)
        c_out = [T(None) for _ in range(16)]
        c_in = [T(None) for _ in range(16)]
        c_o2 = [T(None) for _ in range(44)]
        chain = [T(None) for _ in range(3)]
        nch = [0]

        def pre(out_ap, in_ap, dst):
            ch = chain[nch[0] % 3]
            nch[0] += 1
            k.dma("pool", out_ap, in_ap, writes=[dst, ch])
        for kc in range(16):
            pre(s_out.t[:, :, kc, :].rearrange("n p c -> p n c"),
                w_out.t[layer, kc * 128:(kc + 1) * 128, :].rearrange("p (n c) -> p n c", n=4), c_out[kc])
        f_load_vt(k, vt, gpp, layer, 2, modscr, 3)
        for m in range(2):
            k.stt("dve", G.t[:, m, :], vt.t[:, 2 + 2 * m, :], 1.0, vt.t[:, 0, :], ALU.add, ALU.mult, [vt], [G],
                  part=True)

        def load_GG(m):
            for j in range(2):
                off = layer * 6 * D + (2 + 3 * j) * D
                k.dma("sp", GG[j].t[:], modscr.t[m:m + 1, off:off + D].to_broadcast([128, D]), writes=[GG[j]])
                k.dma("sp", xt[1].t[:], gbc.t[layer, j, :, :], writes=[xt[1]])
                k.tt("dve", GG[j].t[:], GG[j].t[:], xt[1].t[:], ALU.mult, [GG[j], xt[1]], [GG[j]])
        load_GG(0)
        for kc in range(16):
            pre(s_in.t[:, :, kc, :].rearrange("n p c -> p n c"),
                w_in.t[layer, kc * 128:(kc + 1) * 128, :].rearrange("p (n c) -> p n c", n=22), c_in[kc])
        for kc in range(44):
            pre(s_o2.t[:, kc // 11, :, kc % 11, :].rearrange("n p c -> p n c"),
                w_o2.t[layer, kc * 128:(kc + 1) * 128, :].rearrange("p (n c) -> p n c", n=4), c_o2[kc])
        nwb = [0]

        def load_piece(src_ap, nk, deps):
            w = wb[nwb[0] % 4]
            nwb[0] += 1
            k.dma("sp", w.t[:, 0:nk, :], src_ap, reads=deps, writes=[w])
            return w

        cur_m = 0
        for bi, (tile0, ntile, m) in enumerate(blocks):
            ntok = ntile * 128
            if m != cur_m:
                load_GG(m)
                cur_m = m
            for kc in range(16):
                k.dma("sp", mT.t[:, kc, 0:ntok], mixcol(kc, tile0, ntok), writes=[mT], part=(kc > 0))
            for nb in range(4):
                w = load_piece(s_out.t[nb], 16, c_out)
                for tb in range(ntile):
                    p = k.ps()
                    for kc in range(16):
                        k.mm(p.t[:, :], mT.t[:, kc, tb * 128:(tb + 1) * 128], w.t[:, kc, :], kc == 0, kc == 15,
                             [mT, w], [p])
                    k.copy("dve", yb[tb].t[:, nb * 512:(nb + 1) * 512], p.t[:, :], [p], [yb[tb]], part=True)
            for tb in range(ntile):
                ti = tile0 + tb
                s = ti % 2
                sst = ss[s]
                k.op("act", lambda e, a=junk.t[:], b_=yb[tb].t[:], c=sst.t[:, 0:1]: e.activation(
                    out=a, in_=b_, func=AF.Square, accum_out=c), [yb[tb]], [junk, sst])
                rstd_from_ss(k, sst)
                k.dma("sp", xt[s].t[:], xrow(ti), writes=[xt[s]])
                k.stt("dve", yb[tb].t[:], yb[tb].t[:], sst.t[:, 2:3], GG[0].t[:], ALU.mult, ALU.mult,
                      [yb[tb], sst, GG[0]], [yb[tb]])
                k.tt("dve", xt[s].t[:], yb[tb].t[:], xt[s].t[:], ALU.add, [yb[tb], xt[s]], [xt[s]])
                k.dma("pool", orow(ti), xt[s].t[:], reads=[xt[s]], writes=[k.xo_buf], part=True)
                norm_to_hT(k, xt[s], s, m, G.t[:, m, :], vt.t[:, 1 + 2 * m, :], hT, tb, idb, junk, xb, sst, tmp)
            for mg in range(11):
                wg = load_piece(s_in.t[mg], 16, c_in)
                wu = load_piece(s_in.t[11 + mg], 16, c_in)
                for j in range(4):
                    mi = mg * 4 + j
                    pg = k.ps()
                    for kc in range(16):
                        k.mm(pg.t[:, 0:ntok], wg.t[:, kc, j * 128:(j + 1) * 128], hT.t[:, kc, 0:ntok], kc == 0,
                             kc == 15, [wg, hT], [pg])
                    pu = k.ps()
                    for kc in range(16):
                        k.mm(pu.t[:, 0:ntok], wu.t[:, kc, j * 128:(j + 1) * 128], hT.t[:, kc, 0:ntok], kc == 0,
                             kc == 15, [wu, hT], [pu])
                    sg = sgt[mi % 2]
                    k.act(sg.t[:, 0:ntok], pg.t[:, 0:ntok], AF.Silu, [pg], [sg])
                    k.tt("dve", aT.t[:, mi, 0:ntok], sg.t[:, 0:ntok], pu.t[:, 0:ntok], ALU.mult, [sg, pu], [aT],
                         part=True)
            for nb in range(4):
                banks = [k.ps() for _ in range(ntile)]
                for kh in range(4):
                    w = load_piece(s_o2.t[nb, kh], 11, c_o2)
                    for tb in range(ntile):
                        for kk in range(11):
                            k.mm(banks[tb].t[:, :], aT.t[:, kh * 11 + kk, tb * 128:(tb + 1) * 128], w.t[:, kk, :],
                                 kh == 0 and kk == 0, kh == 3 and kk == 10, [aT, w], [banks[tb]])
                for tb in range(ntile):
                    k.copy("dve", yb[tb].t[:, nb * 512:(nb + 1) * 512], banks[tb].t[:, :], [banks[tb]], [yb[tb]],
                           part=True)
            for tb in range(ntile):
                ti = tile0 + tb
                s = ti % 2
                sst = ss[s]
                k.op("act", lambda e, a=junk.t[:], b_=yb[tb].t[:], c=sst.t[:, 0:1]: e.activation(
                    out=a, in_=b_, func=AF.Square, accum_out=c), [yb[tb]], [junk, sst])
                rstd_from_ss(k, sst)
                k.dma("pool", xt[s].t[:], orow(ti), reads=[k.xo_buf], writes=[xt[s]])
                k.stt("dve", yb[tb].t[:], yb[tb].t[:], sst.t[:, 2:3], GG[1].t[:], ALU.mult, ALU.mult,
                      [yb[tb], sst, GG[1]], [yb[tb]])
                k.tt("dve", xt[s].t[:], yb[tb].t[:], xt[s].t[:], ALU.add, [yb[tb], xt[s]], [xt[s]])
                k.dma("pool", orow(ti), xt[s].t[:], reads=[xt[s]], writes=[k.xo_buf], part=True)


def build_fused(split=True):
    k = K()
    x = k.din("x", [SEQ, D], F32)
    cx = k.din("ctx", [CTX, D], F32)
    csT = k.din("csT", [128, 16, 2], F32)
    w_ada = k.din("w_ada", [2, D, 6 * D], F32)
    b_ada = k.din("b_ada", [2, 6 * D], F32)
    gpp = k.din("gpp", [2, 4, 128, 16], F32)
    gbc = k.din("gbc", [2, 2, 128, D], F32)
    w_in = k.din("w_in", [2, D, 3072], F32)
    w_out = k.din("w_out", [2, 2048, D], F32)
    w_fi = k.din("w_ffn_in", [2, D, 2 * DFF], F32)
    w_fo = k.din("w_ffn_out", [2, DFF, D], F32)
    idn = k.din("idn", [128, 128], F32)
    wf = k.din("wf", [2, 2, 128, 2, 2, 256], F32)
    lw = k.din("lw", [2, 2, 128, 16, 256], F32)
    vecs = k.din("vecs", [2, 2, 128, NV], F32)
    cs64 = k.din("cs64", [128, 128], F32)
    rs = k.din("rs", [128, 2, 256], F32)
    c256 = k.din("c256", [128, 2, 512], F32)
    out = k.dout("out", [SEQ // 2 if split else SEQ, D], F32)
    modscr = k.dscr("modscr", [2, 2 * 6 * D], F32)
    X0s = k.dscr("X0s", [2, 4, 128, 64, 128], BF16)
    fctxs = k.dscr("fctxs", [2, 128, 2, 512], BF16)
    rTs = k.dscr("rTs", [2, 128, 4, NTB], BF16)
    gTs = k.dscr("gTs", [2, 128, 4, NTB], BF16)
    mixs = k.dscr("mixs", [2048, NTB], BF16)
    hf_scr = k.dscr("hf_scr", [128, 4, NTB], F32)
    xscr = k.dscr("xscr", [SEQ + CTX, D], F32)
    s_out = k.dscr("s_out", [4, 128, 16, 512], BF16)
    s_in = k.dscr("s_in", [22, 128, 16, 512], BF16)
    s_o2 = k.dscr("s_o2", [4, 4, 128, 11, 512], BF16)
    xh = k.dscr("xh", [SEQ // 2, D], F32)
    mixh = k.dscr("mixh", [2048, SEQ // 2], BF16)
    k.xo_buf = T(None)
    k.psum_pool(8)
    idf = k.sb("idf", [128, 128], F32)
    idb = k.sb("idb", [128, 128], BF16)
    k.dma("sp", idf.t[:], idn.t[:, :], writes=[idf])
    k.copy("dve", idb.t[:], idf.t[:], [idf], [idb])
    f_emit_M(k, csT, w_ada, b_ada, modscr)
    for layer in range(2):
        last = layer == 1
        if layer == 0:
            xrow = lambda ti: (x.t[ti * 128:(ti + 1) * 128, :] if ti < FT_LAT
                               else cx.t[(ti - FT_LAT) * 128:(ti - FT_LAT + 1) * 128, :])
        else:
            xrow = lambda ti: xscr.t[ti * 128:(ti + 1) * 128, :]
        f_emit_A(k, layer, xrow, True, gpp, w_in, idb, modscr, X0s, fctxs, rTs, gTs)
        for hfi in range(2):
            with k.phase():
                emit_B(k, V(X0s.t[hfi], X0s), V(fctxs.t[hfi], fctxs), V(rTs.t[hfi], rTs), V(gTs.t[hfi], gTs),
                       V(wf.t[layer, hfi], wf), V(lw.t[layer, hfi], lw), cs64, rs, c256, V(vecs.t[layer, hfi], vecs),
                       mixs, hf_scr, hfi * 512, 1024 + hfi * 512, idb)
        mixcol = lambda kc, tile0, ntok: mixs.t[kc * 128:(kc + 1) * 128, f_tokcol(tile0):f_tokcol(tile0) + ntok]
        blocks = f_blocks(not last)
        xrow_c = xrow
        if last:
            orow = lambda ti: out.t[ti * 128:(ti + 1) * 128, :]
            if split:
                hh = k.nc.partition_id() % 2
                av = xscr.t[0:SEQ, :].rearrange("(h t) d -> h t d", h=2)
                mv = mixs.t[:, CTX:NTB].rearrange("r (h t) -> r h t", h=2)
                with k.phase():
                    for i in range(4):
                        k.dma("sp", xh.t[i * 1024:(i + 1) * 1024, :],
                              lambda i=i: av[bass.ds(hh, 1), i * 1024:(i + 1) * 1024, :].squeeze(0), writes=[xh],
                              part=True)
                        k.dma("sp", mixh.t[i * 512:(i + 1) * 512, :],
                              lambda i=i: mv[i * 512:(i + 1) * 512, bass.ds(hh, 1), :].squeeze(1), writes=[mixh],
                              part=True)
                blocks = [(t0, 4, 0) for t0 in range(0, FT_LAT // 2, 4)]
                xrow_c = lambda ti: xh.t[ti * 128:(ti + 1) * 128, :]
                mixcol = lambda kc, tile0, ntok: mixh.t[kc * 128:(kc + 1) * 128, tile0 * 128:tile0 * 128 + ntok]
        else:
            orow = lambda ti: xscr.t[ti * 128:(ti + 1) * 128, :]
        f_emit_C(k, layer, blocks, xrow_c, orow, mixcol, gpp, gbc, w_out, w_fi, w_fo, idb, modscr, mixs, s_out, s_in,
                 s_o2)
    k.outs = [out, k.xo_buf]
    return k.finish()


def kernel(x, c, ctx, c_ctx, w_ada, b_ada, norm_g, w_in, w_four, conv_w, conv_b, lru_w, lru_b, lru_lam,
           w_out, w_ffn_in, w_ffn_out):
    f32 = lambda a: np.ascontiguousarray(np.asarray(a, np.float32))
    x, ctx, c, c_ctx, norm_g = f32(x), f32(ctx), f32(c), f32(c_ctx), f32(norm_g)
    shared = {"w_ada": f32(w_ada), "b_ada": f32(b_ada), "w_in": f32(w_in), "w_out": f32(w_out),
              "w_ffn_in": f32(w_ffn_in), "w_ffn_out": f32(w_ffn_out), "idn": np.eye(128, dtype=np.float32)}
    shared.update(_b_consts())
    shared["gpp"] = np.ascontiguousarray(np.stack([np.stack([_pp(norm_g[l, n]) for n in range(4)]) for l in range(2)]))
    shared["gbc"] = np.ascontiguousarray(np.stack([np.stack([np.broadcast_to(norm_g[l, n], (128, D)) for n in (1, 3)])
                                                   for l in range(2)]))
    bw = [[_b_weights(hf, w_four[l], conv_w[l], conv_b[l], lru_w[l], lru_b[l], lru_lam[l]) for hf in range(2)]
          for l in range(2)]
    for key in ("wf", "lw", "vecs"):
        shared[key] = np.ascontiguousarray(np.stack([np.stack([bw[l][hf][key] for hf in range(2)]) for l in range(2)]))
    maps = []
    for core in range(NCORES):
        b = core // 2
        cs = np.stack([c[b], c_ctx], axis=0)
        m = dict(shared)
        m["x"] = x[b]
        m["ctx"] = ctx[b]
        m["csT"] = np.ascontiguousarray(cs.T.reshape(16, 128, 2).transpose(1, 0, 2))
        maps.append(m)
    nc = build_fused(split=True)
    res = run_bass_kernel_spmd(nc, maps, core_ids=list(range(NCORES))).results
    out = np.empty((NB, SEQ, D), np.float32)
    for core in range(NCORES):
        b, h = core // 2, core % 2
        out[b, h * (SEQ // 2):(h + 1) * (SEQ // 2)] = res[core]["out"]
    return out
```
